# Optimizing a Trainium2 kernel written in Bass

```python
import math
import jax, jax.numpy as jnp
from jax import lax
import numpy as np

D_MODEL = 4096
BATCH = 4
SEQ = 4096
DEPTH = 2

N_A_LAYERS = DEPTH // 2
N_B_LAYERS = DEPTH - N_A_LAYERS
CONV_WIDTH = D_MODEL
CONV_KERNEL = 31
MLA_HEADS = 64
Q_LORA_RANK = 1024
KV_LORA_RANK = 512
QK_NOPE_DIM = 128
QK_ROPE_DIM = 64
V_HEAD_DIM = 128
MLA_WIDTH = MLA_HEADS * V_HEAD_DIM
ROPE_BASE = 10000.0
Q_BLOCK = 128
LN_EPS = 1e-5
RMS_EPS = 1e-6
DEEPNORM_ALPHA = (2.0 * DEPTH) ** 0.25
DEEPNORM_BETA = (8.0 * DEPTH) ** -0.25

kernel_name = "yoco_conformer_mla_deepnorm_adaln"


def _layernorm(x, g=None, b=None):
    xf = x.astype(jnp.float32)
    mu = jnp.mean(xf, axis=-1, keepdims=True)
    var = jnp.mean(jnp.square(xf - mu), axis=-1, keepdims=True)
    y = (xf - mu) * lax.rsqrt(var + LN_EPS)
    if g is not None:
        y = y * g.astype(jnp.float32) + b.astype(jnp.float32)
    return y.astype(x.dtype)


def _rmsnorm(x, g):
    xf = x.astype(jnp.float32)
    y = xf * lax.rsqrt(jnp.mean(jnp.square(xf), axis=-1, keepdims=True) + RMS_EPS)
    return (y * g.astype(jnp.float32)).astype(x.dtype)


def _rope(x, pos):
    half = QK_ROPE_DIM // 2
    inv_freq = ROPE_BASE ** (-jnp.arange(half, dtype=jnp.float32) / half)
    ang = pos.astype(jnp.float32)[:, None] * inv_freq[None, :]
    cos = jnp.cos(ang)[None, :, None, :]
    sin = jnp.sin(ang)[None, :, None, :]
    xf = x.astype(jnp.float32)
    x1, x2 = xf[..., :half], xf[..., half:]
    return jnp.concatenate([x1 * cos - x2 * sin, x1 * sin + x2 * cos], axis=-1).astype(x.dtype)


def _modulate(x, c, w_ada, b_ada):
    mod = jax.nn.silu(c) @ w_ada + b_ada
    shift, scale, gate = jnp.split(mod, 3, axis=-1)
    h = _layernorm(x) * (1.0 + scale[:, None, :]) + shift[:, None, :]
    return h, (1.0 + gate)[:, None, :]


def _conformer_conv(h, w_in, w_dw, b_dw, g_cn, b_cn, w_out):
    u = h @ w_in
    a, g, z = jnp.split(u, 3, axis=-1)
    v = a * jax.nn.sigmoid(g)
    v = lax.conv_general_dilated(
        v, w_dw[:, None, :], window_strides=(1,),
        padding=[(CONV_KERNEL - 1, 0)],
        dimension_numbers=("NWC", "WIO", "NWC"),
        feature_group_count=CONV_WIDTH) + b_dw
    v = jax.nn.silu(_layernorm(v, g_cn, b_cn))
    return (v * jax.nn.silu(z)) @ w_out


def _shared_kv(xs, w_kva, g_kv, w_kvb, pos):
    B, S, _ = xs.shape
    kva = xs @ w_kva
    c_kv = _rmsnorm(kva[..., :KV_LORA_RANK], g_kv)
    k_rope = _rope(kva[..., KV_LORA_RANK:][:, :, None, :], pos)[:, :, 0]
    kv = (c_kv @ w_kvb).reshape(B, S, MLA_HEADS, QK_NOPE_DIM + V_HEAD_DIM)
    return kv[..., :QK_NOPE_DIM], k_rope, kv[..., QK_NOPE_DIM:]


def _mla(h, k_nope, k_rope, v, w_in, g_q, w_qb, w_out, pos):
    B, S, _ = h.shape
    u = h @ w_in
    c_q, z = u[..., :Q_LORA_RANK], u[..., Q_LORA_RANK:]
    q = (_rmsnorm(c_q, g_q) @ w_qb).reshape(B, S, MLA_HEADS, QK_NOPE_DIM + QK_ROPE_DIM)
    q_nope = q[..., :QK_NOPE_DIM]
    q_rope = _rope(q[..., QK_NOPE_DIM:], pos)
    scale = (QK_NOPE_DIM + QK_ROPE_DIM) ** -0.5
    outs = []
    for start in range(0, S, Q_BLOCK):
        end = min(start + Q_BLOCK, S)
        s = (jnp.einsum("bqhd,bkhd->bhqk", q_nope[:, start:end], k_nope[:, :end])
             + jnp.einsum("bqhd,bkd->bhqk", q_rope[:, start:end], k_rope[:, :end]))
        s = s.astype(jnp.float32) * scale
        mask = (start + jnp.arange(end - start))[:, None] >= jnp.arange(end)[None, :]
        s = jnp.where(mask[None, None], s, -jnp.inf)
        p = jax.nn.softmax(s, axis=-1).astype(v.dtype)
        outs.append(jnp.einsum("bhqk,bkhd->bqhd", p, v[:, :end]))
    o = jnp.concatenate(outs, axis=1).reshape(B, S, MLA_WIDTH)
    return (o * jax.nn.silu(z)) @ w_out


def setup_inputs(seed: int = 0) -> dict:
    key = jax.random.key(seed)
    ks = jax.random.split(key, 24)
    f32 = jnp.float32
    D = D_MODEL
    nrm = lambda k, shape, s: jax.random.normal(k, shape, f32) * s
    return {
        "x": nrm(ks[0], (BATCH, SEQ, D), 1.0),
        "c": nrm(ks[1], (BATCH, D), 1.0),
        "w_ada": nrm(ks[2], (DEPTH, D, 3 * D), D ** -0.5),
        "b_ada": nrm(ks[3], (DEPTH, 3 * D), 0.02),
        "ln_g": 1.0 + nrm(ks[4], (DEPTH, D), 0.02),
        "ln_b": nrm(ks[5], (DEPTH, D), 0.02),
        "a_w_in": nrm(ks[6], (N_A_LAYERS, D, 3 * CONV_WIDTH), D ** -0.5),
        "a_w_dw": nrm(ks[7], (N_A_LAYERS, CONV_KERNEL, CONV_WIDTH), CONV_KERNEL ** -0.5),
        "a_b_dw": nrm(ks[8], (N_A_LAYERS, CONV_WIDTH), 0.02),
        "a_norm_g": 1.0 + nrm(ks[9], (N_A_LAYERS, CONV_WIDTH), 0.02),
        "a_norm_b": nrm(ks[10], (N_A_LAYERS, CONV_WIDTH), 0.02),
        "a_w_out": nrm(ks[11], (N_A_LAYERS, CONV_WIDTH, D), DEEPNORM_BETA * CONV_WIDTH ** -0.5),
        "b_w_in": nrm(ks[12], (N_B_LAYERS, D, Q_LORA_RANK + MLA_WIDTH), D ** -0.5),
        "b_q_norm_g": 1.0 + nrm(ks[13], (N_B_LAYERS, Q_LORA_RANK), 0.02),
        "b_w_qb": nrm(ks[14], (N_B_LAYERS, Q_LORA_RANK, MLA_HEADS * (QK_NOPE_DIM + QK_ROPE_DIM)), Q_LORA_RANK ** -0.5),
        "b_w_out": nrm(ks[15], (N_B_LAYERS, MLA_WIDTH, D), DEEPNORM_BETA * MLA_WIDTH ** -0.5),
        "kv_w_a": nrm(ks[16], (D, KV_LORA_RANK + QK_ROPE_DIM), D ** -0.5),
        "kv_norm_g": 1.0 + nrm(ks[17], (KV_LORA_RANK,), 0.02),
        "kv_w_b": nrm(ks[18], (KV_LORA_RANK, MLA_HEADS * (QK_NOPE_DIM + V_HEAD_DIM)), KV_LORA_RANK ** -0.5),
    }


def reference(x, c, w_ada, b_ada, ln_g, ln_b, a_w_in, a_w_dw, a_b_dw, a_norm_g, a_norm_b, a_w_out,
              b_w_in, b_q_norm_g, b_w_qb, b_w_out, kv_w_a, kv_norm_g, kv_w_b):
    S = x.shape[1]
    pos = jnp.arange(S, dtype=jnp.int32)
    k_nope = k_rope = v = None
    for layer in range(DEPTH):
        if layer == N_A_LAYERS:
            k_nope, k_rope, v = _shared_kv(x, kv_w_a, kv_norm_g, kv_w_b, pos)
        h, gate = _modulate(x, c, w_ada[layer], b_ada[layer])
        if layer < N_A_LAYERS:
            i = layer
            out = _conformer_conv(h, a_w_in[i], a_w_dw[i], a_b_dw[i], a_norm_g[i], a_norm_b[i], a_w_out[i])
        else:
            j = layer - N_A_LAYERS
            out = _mla(h, k_nope, k_rope, v, b_w_in[j], b_q_norm_g[j], b_w_qb[j], b_w_out[j], pos)
        x = _layernorm(DEEPNORM_ALPHA * x + gate * out, ln_g[layer], ln_b[layer])
    return x
```

```python
import numpy as np
import concourse.bass as bass
import concourse.mybir as mybir
from concourse.bass_utils import run_bass_kernel_spmd

F32 = mybir.dt.float32
BF16 = mybir.dt.bfloat16
U32 = mybir.dt.uint32
AF = mybir.ActivationFunctionType
ALU = mybir.AluOpType
AX = mybir.AxisListType

D = 4096
KD = 32
T = 512
NSLOT = 4
NO_SCATTER = False
NSTEP = 8
SEQ = 4096
BATCH = 4
ALPHA = (2.0 * 2) ** 0.25
LN_EPS = 1e-5
RMS_EPS = 1e-6
TILES = {0: [0, 3, 4, 7], 1: [1, 2, 5, 6]}
PADLEN = [2, 4, 6, 8]
NH = 64
BLK = 512


class Buf:
    __slots__ = ("ap", "res")

    def __init__(self, ap, res):
        self.ap = ap
        self.res = res

    def __getitem__(self, idx):
        return self.ap[idx]


class _Op:
    __slots__ = ("fn", "waits", "sig", "sigval", "dsem", "dtarget")

    def __init__(self, fn):
        self.fn = fn
        self.waits = []
        self.sig = False
        self.sigval = 0
        self.dsem = None
        self.dtarget = 0


class _Res:
    __slots__ = ("w", "r")

    def __init__(self):
        self.w = None
        self.r = {}


ENGS = ("pe", "dve", "act", "pool", "sp")
NDSEM = 12


class Prog:
    def __init__(self, nc):
        self.nc = nc
        self.ops = {e: [] for e in ENGS}
        self.res = {}
        self.waited = {e: {} for e in ENGS}
        self.dcount = {}
        self.drr = {e: 0 for e in ENGS}

    def _get(self, key):
        st = self.res.get(key)
        if st is None:
            st = _Res()
            self.res[key] = st
        return st

    @staticmethod
    def _keys(items):
        out = []
        for it in items:
            if isinstance(it, Buf):
                out.extend(it.res)
            elif isinstance(it, (list, tuple)) and it and isinstance(it[0], (Buf, list)):
                out.extend(Prog._keys(it))
            else:
                out.append(it)
        return out

    def _resolve(self, eng, o, rkeys, wkeys, is_dma):
        need = {}
        for k in rkeys:
            st = self._get(k)
            if st.w is not None:
                self._need(need, st.w)
        for k in wkeys:
            st = self._get(k)
            if st.w is not None:
                self._need(need, st.w)
            for ev in st.r.values():
                self._need(need, ev)
        wd = self.waited[eng]
        for src, ev in need.items():
            if ev[0] == "e":
                if ev[1] == eng and not is_dma:
                    continue
                if wd.get(src, -1) >= ev[2]:
                    continue
                wd[src] = ev[2]
                o.waits.append(ev)
                self.ops[ev[1]][ev[2]].sig = True
            else:
                if wd.get(src, -1) >= ev[2]:
                    continue
                wd[src] = ev[2]
                o.waits.append(ev)

    @staticmethod
    def _need(need, ev):
        src = ev[1]
        cur = need.get(src)
        if cur is None or cur[2] < ev[2]:
            need[src] = ev

    def _record(self, me, rkeys, wkeys):
        for k in rkeys:
            self._get(k).r[me[1]] = me
        for k in wkeys:
            st = self._get(k)
            st.w = me
            st.r = {}

    def op(self, eng, fn, reads=(), writes=(), strict=False):
        rkeys = self._keys(reads)
        wkeys = self._keys(writes)
        o = _Op(fn)
        self._resolve(eng, o, rkeys, wkeys, strict)
        me = ("e", eng, len(self.ops[eng]))
        self.ops[eng].append(o)
        self._record(me, rkeys, wkeys)

    def dma(self, q, out, in_, reads=(), writes=(), fn=None):
        rkeys = self._keys(reads)
        wkeys = self._keys(writes)
        o = _Op(fn if fn is not None else (lambda e: e.dma_start(out=out, in_=in_)))
        self._resolve(q, o, rkeys, wkeys, True)
        k = self.drr[q]
        self.drr[q] = (k + 1) % NDSEM
        src = ("d", q, k)
        prev = self.dcount.get(src, 0)
        if prev and self.waited[q].get(src, -1) < prev:
            self.waited[q][src] = prev
            o.waits.append(("d", src, prev))
        tgt = prev + 16
        self.dcount[src] = tgt
        o.dsem = src
        o.dtarget = tgt
        self.ops[q].append(o)
        self._record(("d", src, tgt), rkeys, wkeys)

    def emit(self):
        nc = self.nc
        import contextlib
        with contextlib.ExitStack() as es:
            esem = {e: es.enter_context(nc.semaphore("s_" + e)) for e in ENGS}
            dsem = {}
            for q in ("sp", "pool", "act"):
                for k in range(NDSEM):
                    dsem[("d", q, k)] = es.enter_context(nc.semaphore("d_%s%d" % (q, k)))
            for e in ENGS:
                c = 0
                for o in self.ops[e]:
                    if o.sig:
                        c += 1
                        o.sigval = c
            fin = _Op(None)
            for src, tgt in self.dcount.items():
                fin.waits.append(("d", src, tgt))
            block = es.enter_context(nc.Block())
            ops = self.ops

            def run(eng_name, eng):
                for o in ops[eng_name]:
                    for ev in o.waits:
                        if ev[0] == "e":
                            eng.wait_ge(esem[ev[1]], ops[ev[1]][ev[2]].sigval)
                        else:
                            eng.wait_ge(dsem[ev[1]], ev[2])
                    ins = o.fn(eng)
                    if o.dsem is not None:
                        ins.then_inc(dsem[o.dsem], 16)
                    elif o.sig:
                        ins.then_inc(esem[eng_name], 1)

            @block.tensor
            def _(t):
                run("pe", t)

            @block.vector
            def _(v):
                run("dve", v)

            @block.scalar
            def _(a):
                run("act", a)

            @block.gpsimd
            def _(g):
                run("pool", g)

            @block.sync
            def _(s):
                run("sp", s)
                for ev in fin.waits:
                    s.wait_ge(dsem[ev[1]], ev[2])


class Sbuf:
    def __init__(self, big, nbytes):
        self.big = big
        self.nbytes = nbytes
        self.top = 0

    def alloc(self, nbytes):
        nbytes = (nbytes + BLK - 1) // BLK * BLK
        off = self.top
        self.top += nbytes
        assert self.top <= self.nbytes, ("sbuf overflow", self.top, self.nbytes)
        return off

    def view(self, off, dtype, shape):
        esz = 4 if dtype == F32 else 2
        n = 1
        for s in shape[1:]:
            n *= s
        nb = n * esz
        assert off % 4 == 0 and nb % 4 == 0
        ap = self.big[:, off // 4:(off + nb) // 4]
        if dtype != F32:
            ap = ap.bitcast(dtype)
        if len(shape) == 3:
            ap = ap.rearrange("p (a b) -> p a b", b=shape[2])
        if shape[0] != 128:
            ap = ap[0:shape[0]]
        res = list(range(off // BLK, (off + nb + BLK - 1) // BLK))
        return Buf(ap, res)

    def new(self, dtype, shape):
        esz = 4 if dtype == F32 else 2
        n = 1
        for s in shape[1:]:
            n *= s
        return self.view(self.alloc(n * esz), dtype, shape)


def sub(buf, ap, lo_bytes=None, hi_bytes=None):
    if lo_bytes is None:
        return Buf(ap, buf.res)
    b0 = buf.res[0]
    return Buf(ap, list(range(b0 + lo_bytes // BLK, b0 + (hi_bytes + BLK - 1) // BLK)))


class Ctx:
    pass


def setup_common(nc, P, S, c):
    c.ps = []
    c.nc = nc
    c.P = P
    c.S = S


def ln_stats(P, c, xs, ntok, mv, st, sd, rstd, nmr, eps_t):
    for i in range(8):
        P.op("dve", lambda e, i=i: e.bn_stats(out=st[0:ntok, i * 6:(i + 1) * 6], in_=xs[0:ntok, i * 512:(i + 1) * 512]),
             reads=[xs], writes=[st])
    P.op("dve", lambda e: e.bn_aggr(out=mv[0:ntok, :], in_=st[0:ntok, :]), reads=[st], writes=[mv], strict=True)
    P.op("act", lambda e: e.activation(out=sd[0:ntok, :], in_=mv[0:ntok, 1:2], func=AF.Sqrt, bias=eps_t[0:ntok, :], scale=1.0),
         reads=[mv, eps_t], writes=[sd])
    P.op("dve", lambda e: e.reciprocal(out=rstd[0:ntok, :], in_=sd[0:ntok, :]), reads=[sd], writes=[rstd])
    P.op("dve", lambda e: e.scalar_tensor_tensor(out=nmr[0:ntok, :], in0=mv[0:ntok, 0:1], scalar=-1.0, in1=rstd[0:ntok, :],
                                                 op0=ALU.mult, op1=ALU.mult), reads=[mv, rstd], writes=[nmr], strict=True)


def stage_a(P, c, x_dram_ap, ntok, xbuf, ybuf, hT_dst, tok0, scale1T, shiftT, xkeys=()):
    P.dma("sp", xbuf[0:ntok, :], x_dram_ap, reads=list(xkeys), writes=[xbuf])
    ln_stats(P, c, xbuf, ntok, c.mv, c.st, c.sd, c.rstd, c.nmr, c.eps5)
    P.op("dve", lambda e: e.tensor_scalar(out=ybuf[0:ntok, :], in0=xbuf[0:ntok, :], scalar1=c.rstd[0:ntok, :],
                                          scalar2=c.nmr[0:ntok, :], op0=ALU.mult, op1=ALU.add),
         reads=[xbuf, c.rstd, c.nmr], writes=[ybuf], strict=True)
    for grp in range(8):
        bank = c.ps[grp % 2]
        psb = bank.ap[:, 0:256].bitcast(BF16)
        for q in range(4):
            j = grp * 4 + q
            P.op("pe", lambda e, j=j, q=q, psb=psb: e.transpose(out=psb[:, q * 128:q * 128 + ntok],
                                                                  in_=ybuf[0:ntok, j * 128:(j + 1) * 128],
                                                                  identity=c.identb[0:ntok, 0:ntok]),
                 reads=[ybuf, c.identb], writes=[bank])
        for q in range(4):
            j = grp * 4 + q
            dst = hT_dst(j)
            P.op("act", lambda e, j=j, q=q, psb=psb, dst=dst: e.activation(
                out=dst.ap, in_=psb[:, q * 128:q * 128 + ntok], func=AF.Identity,
                scale=scale1T[:, j:j + 1], bias=shiftT[:, j:j + 1]),
                 reads=[bank, scale1T, shiftT], writes=[dst])


class WRing:
    def __init__(self, P, S, nslots, slot_elems):
        self.P = P
        self.slots = [S.new(BF16, [128, slot_elems]) for _ in range(nslots)]
        self.i = 0
        self.slot_elems = slot_elems

    def load(self, dram_ap, nelem, slot=None):
        if slot is None:
            s = self.slots[self.i % len(self.slots)]
            self.i += 1
        else:
            s = self.slots[slot]
        dst = Buf(s.ap[:, 0:nelem], s.res)
        b = 2048
        while nelem % b:
            b -= 128
        if nelem > b:
            o3 = dst.ap.rearrange("p (a b) -> p a b", b=b)
            i3 = dram_ap.rearrange("p (a b) -> p a b", b=b)
        else:
            o3, i3 = dst.ap, dram_ap
        self.P.dma("pool", o3, i3, reads=[], writes=[dst])
        return dst


def compute_mod(P, c, ring, wada, bada, cT_d, gate_d, shiftT, scale1T):
    S = c.S
    cT = c.tmpA
    P.dma("sp", cT.ap, cT_d, writes=[cT])
    scb = c.scb
    P.op("act", lambda e: e.activation(out=scb.ap, in_=cT.ap.unsqueeze(2).broadcast_to([128, 32, 128]), func=AF.Silu),
         reads=[cT], writes=[scb])
    for n in range(24):
        bank = c.ps[n % 8]
        bp = c.piece[n % 2]
        P.dma("sp", bp.ap, bada[:, n * 512:(n + 1) * 512], writes=[bp])
        for kg in range(2):
            w = ring.load(wada[n * 2 + kg], 16 * 512)
            w3 = w.ap.rearrange("p (a b) -> p a b", b=512)
            for kc in range(16):
                k = kg * 16 + kc
                P.op("pe", lambda e, k=k, kc=kc, w3=w3, bank=bank: e.matmul(
                    bank.ap, lhsT=scb.ap[:, k, :], rhs=w3[:, kc, :], start=(k == 0), stop=(k == 31)),
                     reads=[scb, w], writes=[bank])
        mp = c.modp[n % 2]
        if n < 16:
            P.op("dve", lambda e, bank=bank, bp=bp, mp=mp: e.tensor_tensor(out=mp.ap, in0=bank.ap, in1=bp.ap, op=ALU.add),
                 reads=[bank, bp], writes=[mp])
            dg = c.dg
            P.op("dve", lambda e, mp=mp, dg=dg: e.tensor_tensor(
                out=dg.ap, in0=mp.ap.rearrange("p (a b) -> p a b", b=128),
                in1=c.identf.ap.unsqueeze(1).broadcast_to([128, 4, 128]), op=ALU.mult),
                 reads=[mp, c.identf], writes=[dg])
            dst = shiftT if n < 8 else scale1T
            col = (n % 8) * 4
            P.op("dve", lambda e, dg=dg, dst=dst, col=col: e.tensor_reduce(
                out=dst.ap[:, col:col + 4], in_=dg.ap, op=ALU.add, axis=AX.X),
                 reads=[dg], writes=[dst])
        else:
            P.op("dve", lambda e, bank=bank, bp=bp, mp=mp: e.scalar_tensor_tensor(
                out=mp.ap, in0=bank.ap, scalar=1.0, in1=bp.ap, op0=ALU.add, op1=ALU.add),
                 reads=[bank, bp], writes=[mp])
            col = (n - 16) * 512
            P.dma("sp", gate_d[:, col:col + 512], mp.ap, reads=[mp], writes=[("dram", "gate", n - 16)])
    P.op("dve", lambda e: e.tensor_scalar(out=scale1T.ap, in0=scale1T.ap, scalar1=1.0, scalar2=None, op0=ALU.add),
         reads=[scale1T], writes=[scale1T], strict=True)


def final_ln(P, c, r_all, g_d, b_d, nsub=4):
    for s in range(nsub):
        r = r_all[s]
        ln_stats(P, c, r, 128, c.mv4[s], c.st, c.sd, c.rstd4[s], c.nmr4[s], c.eps5)
        P.op("act", lambda e, r=r, s=s: e.activation(out=r.ap, in_=r.ap, func=AF.Identity,
                                                   scale=c.rstd4[s].ap, bias=c.nmr4[s].ap),
             reads=[r, c.rstd4[s], c.nmr4[s]], writes=[r])
    for n in range(8):
        gp = (c.piece[0], c.modp[0])[n % 2]
        bp = (c.piece[1], c.modp[1])[n % 2]
        P.dma("sp", gp.ap, g_d[:, n * 512:(n + 1) * 512], writes=[gp])
        P.dma("sp", bp.ap, b_d[:, n * 512:(n + 1) * 512], writes=[bp])
        for s in range(nsub):
            rs = sub(r_all[s], r_all[s].ap[:, n * 512:(n + 1) * 512], n * 2048, (n + 1) * 2048)
            P.op("dve", lambda e, rs=rs, gp=gp: e.tensor_tensor(out=rs.ap, in0=rs.ap, in1=gp.ap, op=ALU.mult),
                 reads=[rs, gp], writes=[rs])
            P.op("dve", lambda e, rs=rs, bp=bp: e.tensor_tensor(out=rs.ap, in0=rs.ap, in1=bp.ap, op=ALU.add),
                 reads=[rs, bp], writes=[rs])


def outproj_residual(P, c, ring, wtiles, nkg, lhs_chunk, r_all, gate_d, gate_keys):
    for n in range(8):
        gp = c.piece[n % 2]
        P.dma("sp", gp.ap, gate_d[:, n * 512:(n + 1) * 512], reads=[gate_keys[n]], writes=[gp])
        banks = [c.ps[(n % 2) * 4 + s] for s in range(4)]
        for kg in range(nkg):
            w = ring.load(wtiles[n * nkg + kg], 16 * 512)
            w3 = w.ap.rearrange("p (a b) -> p a b", b=512)
            for s in range(4):
                for kc in range(16):
                    k = kg * 16 + kc
                    lh = lhs_chunk(k, s)
                    P.op("pe", lambda e, lh=lh, kc=kc, w3=w3, bank=banks[s], k=k: e.matmul(
                        bank.ap, lhsT=lh.ap, rhs=w3[:, kc, :], start=(k == 0), stop=(k == nkg * 16 - 1)),
                         reads=[lh, w], writes=[banks[s]])
        for s in range(4):
            tmp = c.modp[s % 2]
            rs = sub(r_all[s], r_all[s].ap[:, n * 512:(n + 1) * 512], n * 2048, (n + 1) * 2048)
            P.op("dve", lambda e, bank=banks[s], gp=gp, tmp=tmp: e.tensor_tensor(out=tmp.ap, in0=bank.ap, in1=gp.ap, op=ALU.mult),
                 reads=[banks[s], gp], writes=[tmp])
            P.op("dve", lambda e, rs=rs, tmp=tmp: e.scalar_tensor_tensor(out=rs.ap, in0=rs.ap, scalar=ALPHA, in1=tmp.ap,
                                                                       op0=ALU.mult, op1=ALU.add),
                 reads=[rs, tmp], writes=[rs])


def alloc_small(c, S):
    a = S.alloc(512)
    c.mv = S.view(a, F32, [128, 2])
    c.sd = S.view(a + 8, F32, [128, 1])
    c.rstd = S.view(a + 12, F32, [128, 1])
    c.nmr = S.view(a + 16, F32, [128, 1])
    c.st = S.view(a + 64, F32, [128, 48])
    b = S.alloc(512)
    c.mv4 = [S.view(b + 8 * i, F32, [128, 2]) for i in range(4)]
    c.rstd4 = [S.view(b + 32 + 4 * i, F32, [128, 1]) for i in range(4)]
    c.nmr4 = [S.view(b + 48 + 4 * i, F32, [128, 1]) for i in range(4)]
    k = S.alloc(512)
    c.eps5 = S.view(k, F32, [128, 1])
    c.eps6 = S.view(k + 4, F32, [128, 1])
    c.identb = S.new(BF16, [128, 128])
    c.identf = S.new(F32, [128, 128])
    c.onesf = S.new(F32, [128, 128])
    m = S.alloc(512)
    c.shiftT = S.view(m, F32, [128, 32])
    c.scale1T = S.view(m + 128, F32, [128, 32])
    c.tmpA = S.view(m + 256, F32, [128, 32])
    c.dg = S.new(F32, [128, 4, 128])
    c.piece = [S.new(F32, [128, 512]) for _ in range(2)]
    c.modp = [S.new(F32, [128, 512]) for _ in range(2)]


def init_consts(P, c, ident_d):
    P.dma("sp", c.identf.ap, ident_d, writes=[c.identf])
    P.dma("pool", c.identb.ap, ident_d, writes=[c.identb])
    P.op("dve", lambda e: e.memset(c.eps5.ap, LN_EPS), writes=[c.eps5])
    P.op("dve", lambda e: e.memset(c.eps6.ap, RMS_EPS), writes=[c.eps6])
    P.op("dve", lambda e: e.memset(c.onesf.ap, 1.0), writes=[c.onesf])


L0_IN = {"xt": [NSTEP, T, D], "xh": [NSTEP, 32, D], "hmask": [128, NSTEP], "cT": [128, 32], "wada": [48, 128, 8192],
         "bada": [128, 3 * D], "win": [48, 128, 8192], "wdw": [128, 32 * 31], "bdw": [128, 32], "cng": [128, 32],
         "cnb": [128, 32], "wout": [16, 128, 8192], "lng": [128, D], "lnb": [128, D], "kvwa": [4, 128, 8 * 576],
         "kvg": [128, 512], "cs": [NSTEP, 4, 128, 64], "ident": [128, 128]}
SB_BYTES = 183 * 1024


def l0_body(nc, P, big, banks, t, nslot=NSTEP):
    xt, xh, hmask_d, cT_d, wada, bada, win = t["xt"], t["xh"], t["hmask"], t["cT"], t["wada"], t["bada"], t["win"]
    wdw_d, bdw_d, cng_d, cnb_d, wout, lng_d, lnb_d = t["wdw"], t["bdw"], t["cng"], t["cnb"], t["wout"], t["lng"], t["lnb"]
    kvwa, kvg_d, cs_d, ident_d = t["kvwa"], t["kvg"], t["cs"], t["ident"]
    x1, lat_all, idx_d, gate_d = t["x1"], t["lat_all"], t["idx"], t["gate_s"]
    if True:
        S = Sbuf(big, SB_BYTES)
        c = Ctx()
        setup_common(nc, P, S, c)
        c.ps = [Buf(b[:], [("ps", i)]) for i, b in enumerate(banks)]
        alloc_small(c, S)
        wdw = S.new(F32, [128, 32 * 31])
        kk = S.alloc(512)
        bdw = S.view(kk, F32, [128, 32])
        cng = S.view(kk + 128, F32, [128, 32])
        cnb = S.view(kk + 256, F32, [128, 32])
        hmask = S.view(kk + 384, F32, [128, NSTEP])
        idx_sb = Buf(t["idx_sb"][:], [("sb", "idx")])
        kvg = S.new(F32, [128, 512])
        meanb = S.new(F32, [128, 512])
        rstdb = S.new(F32, [128, 512])
        ring = WRing(P, S, 2, 8192)
        cvog_off = S.alloc(32 * 1024)
        cvog = S.view(cvog_off, BF16, [128, 32, 512])
        r1 = S.alloc(64 * 1024)
        hT = S.view(r1, BF16, [128, 32, 512])
        xbuf = S.view(r1 + 32 * 1024, F32, [128, D])
        ybuf = S.view(r1 + 48 * 1024, BF16, [128, D])
        hTh = S.view(r1 + 56 * 1024, BF16, [128, 32, 32])
        c.scb = S.view(r1, BF16, [128, 32, 128])
        r_all = [S.view(r1 + s * 16 * 1024, F32, [128, D]) for s in range(4)]
        r2 = S.alloc(28 * 1024)
        o = r2
        sig = []; vb = []; acc = []; sq = []
        for i in range(2):
            sig.append(S.view(o, F32, [128, 544])); o += 544 * 4
            vb.append(S.view(o, F32, [128, 544])); o += 544 * 4
            acc.append(S.view(o, F32, [128, 512])); o += 2048
            sq.append(S.view(o, F32, [128, 512])); o += 2048
        assert o - r2 <= 18 * 1024
        sz = [S.view(r2 + i * 6144, F32, [128, 512]) for i in range(2)]
        n1 = [S.view(r2 + i * 6144 + 2048, F32, [128, 512]) for i in range(2)]
        sact = [S.view(r2 + i * 6144 + 4096, F32, [128, 512]) for i in range(2)]
        x1b = S.view(r2, BF16, [128, D])
        lat_tok = S.view(r2 + 8192, F32, [128, 640])
        latTs = S.view(r2 + 18 * 1024, F32, [128, 5, 512])
        junk = S.view(r2 + 8192 + 5120, F32, [128, 512])
        ropet = S.view(r2 + 8192 + 5120 + 2048, F32, [128, 4 * 32])
        cs = S.view(r2 + 8192 + 5120 + 2048 + 512, F32, [128, 4 * 64])
        ss = S.view(r2 + 8192 + 5120 + 2048 + 512 + 1024, F32, [128, 1])
        rinv = S.view(r2 + 8192 + 5120 + 2048 + 512 + 1024 + 512, F32, [128, 1])
        x1T = cvog
        print("L0 sbuf bytes used", S.top)

        init_consts(P, c, ident_d)
        P.dma("sp", wdw.ap, wdw_d, writes=[wdw])
        P.dma("sp", bdw.ap, bdw_d, writes=[bdw])
        P.dma("sp", cng.ap, cng_d, writes=[cng])
        P.dma("sp", cnb.ap, cnb_d, writes=[cnb])
        P.dma("sp", hmask.ap, hmask_d, writes=[hmask])
        P.dma("sp", idx_sb.ap, idx_d, writes=[idx_sb])
        P.dma("sp", kvg.ap, kvg_d, writes=[kvg])

        compute_mod(P, c, ring, wada, bada, cT_d, gate_d, c.shiftT, c.scale1T)
        gate_keys = [("dram", "gate", n) for n in range(8)]

        for slot in range(nslot):
            for s in range(4):
                stage_a(P, c, xt[slot, s * 128:(s + 1) * 128, :], 128, xbuf, ybuf,
                        lambda j, s=s: sub(hT, hT.ap[:, j, s * 128:(s + 1) * 128], j * 1024 + s * 256, j * 1024 + (s + 1) * 256),
                        s * 128, c.scale1T, c.shiftT)
            stage_a(P, c, xh[slot], 32, xbuf, ybuf,
                    lambda j: sub(hTh, hTh.ap[:, j, :], j * 64, (j + 1) * 64), 0, c.scale1T, c.shiftT)
            pending_stats = []
            for ch in range(32):
                w = ring.load(win[ch], 8192)
                w3 = w.ap.rearrange("p (a b) -> p a b", b=256)
                pa = c.ps[2 + 2 * (ch % 2)]
                pg = c.ps[3 + 2 * (ch % 2)]
                ph = c.ps[ch % 2]
                for (bank, coff) in ((pg, 128), (pa, 0)):
                    for k in range(32):
                        hk = sub(hT, hT.ap[:, k, :], k * 1024, (k + 1) * 1024)
                        P.op("pe", lambda e, bank=bank, coff=coff, k=k, w3=w3, hk=hk: e.matmul(
                            bank.ap, lhsT=w3[:, k, coff:coff + 128], rhs=hk.ap, start=(k == 0), stop=(k == 31)),
                             reads=[w, hk], writes=[bank])
                for (col0, coff) in ((0, 0), (32, 128)):
                    for k in range(32):
                        hk = sub(hTh, hTh.ap[:, k, :], k * 64, (k + 1) * 64)
                        P.op("pe", lambda e, col0=col0, coff=coff, k=k, w3=w3, hk=hk, ph=ph: e.matmul(
                            ph.ap[:, col0:col0 + 32], lhsT=w3[:, k, coff:coff + 128], rhs=hk.ap, start=(k == 0), stop=(k == 31)),
                             reads=[w, hk], writes=[ph])
                for fn_ in pending_stats:
                    fn_()
                pending_stats = []
                i2 = ch % 2
                sg, v, ac, sqq = sig[i2], vb[i2], acc[i2], sq[i2]
                P.op("act", lambda e, sg=sg, pg=pg: e.activation(out=sg.ap[:, 32:544], in_=pg.ap, func=AF.Sigmoid),
                     reads=[pg], writes=[sg])
                P.op("act", lambda e, sg=sg, ph=ph: e.activation(out=sg.ap[:, 0:32], in_=ph.ap[:, 32:64], func=AF.Sigmoid),
                     reads=[ph], writes=[sg])
                P.op("dve", lambda e, v=v, pa=pa, sg=sg: e.tensor_tensor(out=v.ap[:, 32:544], in0=pa.ap, in1=sg.ap[:, 32:544], op=ALU.mult),
                     reads=[pa, sg], writes=[v])
                P.op("dve", lambda e, v=v, ph=ph, sg=sg, slot=slot: e.scalar_tensor_tensor(
                    out=v.ap[:, 0:32], in0=ph.ap[:, 0:32], scalar=hmask.ap[:, slot:slot + 1], in1=sg.ap[:, 0:32],
                    op0=ALU.mult, op1=ALU.mult), reads=[ph, sg, hmask], writes=[v])
                P.op("dve", lambda e, v=v, ac=ac, ch=ch: e.tensor_scalar(
                    out=ac.ap, in0=v.ap[:, 2:514], scalar1=wdw.ap[:, ch * 31:ch * 31 + 1], scalar2=bdw.ap[:, ch:ch + 1],
                    op0=ALU.mult, op1=ALU.add), reads=[v, wdw, bdw], writes=[ac])
                for k in range(1, 31):
                    P.op("dve", lambda e, v=v, ac=ac, ch=ch, k=k: e.scalar_tensor_tensor(
                        out=ac.ap, in0=v.ap[:, 2 + k:514 + k], scalar=wdw.ap[:, ch * 31 + k:ch * 31 + k + 1], in1=ac.ap,
                        op0=ALU.mult, op1=ALU.add), reads=[v, ac], writes=[ac])
                P.op("act", lambda e, ac=ac, sqq=sqq: e.activation(out=sqq.ap, in_=ac.ap, func=AF.Square),
                     reads=[ac], writes=[sqq])
                cv = sub(cvog, cvog.ap[:, ch, :], ch * 1024, (ch + 1) * 1024)
                P.op("act", lambda e, ac=ac, cv=cv: e.activation(out=cv.ap, in_=ac.ap, func=AF.Identity),
                     reads=[ac], writes=[cv])
                pending_stats.append(lambda ac=ac, ch=ch: P.op(
                    "pe", lambda e: e.matmul(c.ps[6].ap, lhsT=c.onesf.ap, rhs=ac.ap, start=(ch == 0), stop=(ch == 31)),
                    reads=[ac, c.onesf], writes=[c.ps[6]]))
                pending_stats.append(lambda sqq=sqq, ch=ch: P.op(
                    "pe", lambda e: e.matmul(c.ps[7].ap, lhsT=c.onesf.ap, rhs=sqq.ap, start=(ch == 0), stop=(ch == 31)),
                    reads=[sqq, c.onesf], writes=[c.ps[7]]))
            for fn_ in pending_stats:
                fn_()
            pending_stats = []
            P.op("dve", lambda e: e.tensor_scalar(out=meanb.ap, in0=c.ps[6].ap, scalar1=1.0 / D, scalar2=None, op0=ALU.mult),
                 reads=[c.ps[6]], writes=[meanb])
            P.op("dve", lambda e: e.tensor_tensor(out=c.modp[0].ap, in0=meanb.ap, in1=meanb.ap, op=ALU.mult),
                 reads=[meanb], writes=[c.modp[0]])
            P.op("dve", lambda e: e.scalar_tensor_tensor(out=c.modp[1].ap, in0=c.ps[7].ap, scalar=1.0 / D, in1=c.modp[0].ap,
                                                         op0=ALU.mult, op1=ALU.subtract),
                 reads=[c.ps[7], c.modp[0]], writes=[c.modp[1]])
            P.op("act", lambda e: e.activation(out=c.modp[0].ap, in_=c.modp[1].ap, func=AF.Sqrt, bias=c.eps5.ap, scale=1.0),
                 reads=[c.modp[1], c.eps5], writes=[c.modp[0]])
            P.op("dve", lambda e: e.reciprocal(out=rstdb.ap, in_=c.modp[0].ap), reads=[c.modp[0]], writes=[rstdb])
            for zt in range(16):
                w = ring.load(win[32 + zt], 8192)
                w3 = w.ap.rearrange("p (a b) -> p a b", b=256)
                for hh in range(2):
                    ch = zt * 2 + hh
                    pz = c.ps[2 + (ch % 4)]
                    for k in range(32):
                        hk = sub(hT, hT.ap[:, k, :], k * 1024, (k + 1) * 1024)
                        P.op("pe", lambda e, pz=pz, hh=hh, k=k, w3=w3, hk=hk: e.matmul(
                            pz.ap, lhsT=w3[:, k, hh * 128:(hh + 1) * 128], rhs=hk.ap, start=(k == 0), stop=(k == 31)),
                             reads=[w, hk], writes=[pz])
                    i2 = ch % 2
                    cv = sub(cvog, cvog.ap[:, ch, :], ch * 1024, (ch + 1) * 1024)
                    P.op("act", lambda e, pz=pz, i2=i2: e.activation(out=sz[i2].ap, in_=pz.ap, func=AF.Silu),
                         reads=[pz], writes=[sz[i2]])
                    P.op("dve", lambda e, cv=cv, i2=i2: e.tensor_tensor(out=n1[i2].ap, in0=cv.ap, in1=meanb.ap, op=ALU.subtract),
                         reads=[cv, meanb], writes=[n1[i2]])
                    P.op("dve", lambda e, i2=i2: e.tensor_tensor(out=n1[i2].ap, in0=n1[i2].ap, in1=rstdb.ap, op=ALU.mult),
                         reads=[n1[i2], rstdb], writes=[n1[i2]])
                    P.op("act", lambda e, i2=i2, ch=ch: e.activation(out=sact[i2].ap, in_=n1[i2].ap, func=AF.Silu,
                                                                   scale=cng.ap[:, ch:ch + 1], bias=cnb.ap[:, ch:ch + 1]),
                         reads=[n1[i2], cng, cnb], writes=[sact[i2]])
                    P.op("dve", lambda e, cv=cv, i2=i2: e.tensor_tensor(out=cv.ap, in0=sact[i2].ap, in1=sz[i2].ap, op=ALU.mult),
                         reads=[sact[i2], sz[i2]], writes=[cv])
            for s in range(4):
                P.dma("sp", r_all[s].ap, xt[slot, s * 128:(s + 1) * 128, :], writes=[r_all[s]])
            outproj_residual(P, c, ring, wout, 2,
                             lambda k, s: sub(cvog, cvog.ap[:, k, s * 128:(s + 1) * 128], k * 1024 + s * 256, k * 1024 + (s + 1) * 256),
                             r_all, gate_d, gate_keys)
            final_ln(P, c, r_all, lng_d, lnb_d)
            for s in range(4):
                P.dma("sp", x1[slot, s * 128:(s + 1) * 128, :], r_all[s].ap, reads=[r_all[s]], writes=[("dram", "x1", slot, s)])
            P.dma("sp", cs.ap.rearrange("p (a b) -> p a b", b=64), cs_d[slot].rearrange("a p b -> p a b"), writes=[cs])
            for s in range(4):
                P.op("act", lambda e, s=s: e.activation(out=x1b.ap, in_=r_all[s].ap, func=AF.Identity), reads=[r_all[s]], writes=[x1b])
                for grp in range(8):
                    bank = c.ps[grp % 2]
                    psb = bank.ap[:, 0:256].bitcast(BF16)
                    for q in range(4):
                        j = grp * 4 + q
                        P.op("pe", lambda e, j=j, q=q, psb=psb: e.transpose(out=psb[:, q * 128:(q + 1) * 128],
                                                                              in_=x1b.ap[:, j * 128:(j + 1) * 128],
                                                                              identity=c.identb.ap),
                             reads=[x1b, c.identb], writes=[bank])
                    dst = Buf(x1T.ap[:, grp * 4:grp * 4 + 4, s * 128:(s + 1) * 128],
                              list(range(x1T.res[0] + grp * 8, x1T.res[0] + grp * 8 + 8)))
                    P.op("dve", lambda e, psb=psb, dst=dst: e.tensor_copy(out=dst.ap, in_=psb.rearrange("p (a b) -> p a b", b=128)),
                         reads=[bank], writes=[dst])
            for kg in range(4):
                w = ring.load(kvwa[kg], 8 * 576)
                w3 = w.ap.rearrange("p (a b) -> p a b", b=576)
                for s in range(4):
                    for kc in range(8):
                        k = kg * 8 + kc
                        lh = sub(x1T, x1T.ap[:, k, s * 128:(s + 1) * 128], k * 1024 + s * 256, k * 1024 + (s + 1) * 256)
                        P.op("pe", lambda e, s=s, k=k, kc=kc, w3=w3, lh=lh: e.matmul(
                            c.ps[s].ap, lhsT=lh.ap, rhs=w3[:, kc, 0:512], start=(k == 0), stop=(k == 31)),
                             reads=[lh, w], writes=[c.ps[s]])
                        P.op("pe", lambda e, s=s, k=k, kc=kc, w3=w3, lh=lh: e.matmul(
                            c.ps[4 + s].ap[:, 0:64], lhsT=lh.ap, rhs=w3[:, kc, 512:576], start=(k == 0), stop=(k == 31)),
                             reads=[lh, w], writes=[c.ps[4 + s]])
            for s in range(4):
                pa, pb = c.ps[s], c.ps[4 + s]
                P.op("dve", lambda e: e.memset(lat_tok.ap[:, 576:640], 0.0), writes=[lat_tok])
                P.op("act", lambda e, pa=pa: e.activation(out=junk.ap, in_=pa.ap, func=AF.Square, accum_out=ss.ap),
                     reads=[pa], writes=[junk, ss])
                P.op("act", lambda e: e.activation(out=ss.ap, in_=ss.ap, func=AF.Sqrt, bias=c.eps6.ap, scale=1.0 / 512),
                     reads=[ss, c.eps6], writes=[ss], strict=True)
                P.op("dve", lambda e: e.reciprocal(out=rinv.ap, in_=ss.ap), reads=[ss], writes=[rinv])
                P.op("dve", lambda e, pa=pa: e.scalar_tensor_tensor(out=lat_tok.ap[:, 0:512], in0=pa.ap, scalar=rinv.ap[:, 0:1],
                                                                  in1=kvg.ap, op0=ALU.mult, op1=ALU.mult),
                     reads=[pa, rinv, kvg], writes=[lat_tok], strict=True)
                cosv = cs.ap[:, s * 64:s * 64 + 32]
                sinv = cs.ap[:, s * 64 + 32:s * 64 + 64]
                rt = ropet.ap
                P.op("dve", lambda e, pb=pb, cosv=cosv, rt=rt: e.tensor_tensor(out=rt[:, 0:32], in0=pb.ap[:, 0:32], in1=cosv, op=ALU.mult),
                     reads=[pb, cs], writes=[ropet])
                P.op("dve", lambda e, pb=pb, sinv=sinv, rt=rt: e.tensor_tensor(out=rt[:, 32:64], in0=pb.ap[:, 32:64], in1=sinv, op=ALU.mult),
                     reads=[pb, cs], writes=[ropet])
                P.op("dve", lambda e, rt=rt: e.tensor_tensor(out=lat_tok.ap[:, 512:544], in0=rt[:, 0:32], in1=rt[:, 32:64], op=ALU.subtract),
                     reads=[ropet], writes=[lat_tok])
                P.op("dve", lambda e, pb=pb, sinv=sinv, rt=rt: e.tensor_tensor(out=rt[:, 64:96], in0=pb.ap[:, 0:32], in1=sinv, op=ALU.mult),
                     reads=[pb, cs], writes=[ropet])
                P.op("dve", lambda e, pb=pb, cosv=cosv, rt=rt: e.tensor_tensor(out=rt[:, 96:128], in0=pb.ap[:, 32:64], in1=cosv, op=ALU.mult),
                     reads=[pb, cs], writes=[ropet])
                P.op("dve", lambda e, rt=rt: e.tensor_tensor(out=lat_tok.ap[:, 544:576], in0=rt[:, 64:96], in1=rt[:, 96:128], op=ALU.add),
                     reads=[ropet], writes=[lat_tok])
                for j in range(5):
                    bank = c.ps[s] if j < 4 else c.ps[4 + s]
                    col = (j % 4) * 128
                    P.op("pe", lambda e, j=j, bank=bank, col=col: e.transpose(out=bank.ap[:, col:col + 128],
                                                                              in_=lat_tok.ap[:, j * 128:(j + 1) * 128],
                                                                              identity=c.identf.ap),
                         reads=[lat_tok, c.identf], writes=[bank])
                st4 = Buf(latTs.ap[:, 0:4, s * 128:(s + 1) * 128], latTs.res)
                st1 = Buf(latTs.ap[:, 4, s * 128:(s + 1) * 128], latTs.res)
                P.op("act", lambda e, pa=pa, st4=st4: e.activation(out=st4.ap, in_=pa.ap.rearrange("p (j t) -> p j t", t=128), func=AF.Identity),
                     reads=[pa], writes=[st4])
                P.op("act", lambda e, pb=pb, st1=st1: e.activation(out=st1.ap, in_=pb.ap[:, 0:128], func=AF.Identity), reads=[pb], writes=[st1])
            for j in range(5 if not NO_SCATTER else 0):
                k = slot * 5 + j
                P.dma("pool", None, None, reads=[latTs, idx_sb], writes=[("dram", "lat", slot, j), ("scatter_token",)],
                      fn=lambda e, j=j, k=k: e.indirect_dma_start(out=lat_all, out_offset=bass.IndirectOffsetOnAxis(idx_sb.ap[:, k:k + 1], 0),
                                                                  in_=latTs.ap[:, j, :], in_offset=None,
                                                                  bounds_check=e.to_reg(8 * 640 - 1), oob_is_err=False))
        pass


L1_IN = {"wada1": [48, 128, 8192], "bada1": [128, 3 * D], "bwin": [36, 128, 8192], "qg": [128, 8],
         "wqb": [16, 128, 8 * 768], "kvwb": [8, 128, 4 * 2048], "wout1": [32, 128, 8192], "lng1": [128, D], "lnb1": [128, D],
         "csq": [NSLOT, 64, 2, T], "mask": [4, 128, T], "mflag": [128, 2]}


def l1_body(nc, P, big, banks, t, nslot=NSLOT, nheads=NH):
    xt, lat_all, cT_d, wada, bada, bwin, qg_d = t["x1"], t["lat_all"], t["cT"], t["wada1"], t["bada1"], t["bwin"], t["qg"]
    wqb, kvwb, wout, lng_d, lnb_d, csq_d, mask_d, mflag_d = t["wqb"], t["kvwb"], t["wout1"], t["lng1"], t["lnb1"], t["csq"], t["mask"], t["mflag"]
    ident_d, out_d, gate_d, sz_d, og_d = t["ident"], t["out"], t["gate_s1"], t["sz_s"], t["og_s"]
    QSCALE = 192.0 ** -0.5
    if True:
        S = Sbuf(big, SB_BYTES)
        c = Ctx()
        setup_common(nc, P, S, c)
        c.ps = [Buf(b[:], [("ps", i)]) for i, b in enumerate(banks)]
        alloc_small(c, S)
        qg = S.new(F32, [128, 8])
        onesb = S.new(BF16, [128, 128])
        ring = WRing(P, S, 2, 8192)
        KB = 1024
        r0 = S.alloc(S.nbytes - S.top - 512)
        print("L1 pool bytes", S.nbytes - r0)
        hT = S.view(r0, BF16, [128, 32, 512])
        xbuf = S.view(r0 + 32 * KB, F32, [128, D])
        ybuf = S.view(r0 + 48 * KB, BF16, [128, D])
        cqf = S.view(r0 + 56 * KB, F32, [128, 8, 512])
        cqnT = S.view(r0 + 72 * KB, BF16, [128, 8, NSLOT * T])
        zst = [S.view(r0 + 104 * KB + i * KB, BF16, [128, 512]) for i in range(2)]
        sqb = S.view(r0 + 106 * KB, F32, [128, 512])
        rinvb = S.view(r0 + 108 * KB, F32, [128, 512])
        c.scb = S.view(r0, BF16, [128, 32, 128])
        latb = S.view(r0, BF16, [128, 5, SEQ])
        kTh = S.view(r0 + 40 * KB, BF16, [128, SEQ])
        Vh = S.view(r0 + 48 * KB, BF16, [128, 32, 128])
        qn = S.view(r0 + 56 * KB, BF16, [128, NSLOT * T])
        qr = S.view(r0 + 60 * KB, BF16, [128, NSLOT * T])
        ropeT = S.view(r0 + 64 * KB, F32, [128, 4 * 512])
        pT = [S.view(r0 + 104 * KB + i * KB, BF16, [128, 512]) for i in range(2)]
        masks = S.view(r0 + 106 * KB, BF16, [128, 4, 512])
        mflag = S.view(r0 + 110 * KB, F32, [128, 2])
        szh = S.view(r0 + 114 * KB, BF16, [128, NSLOT * T])
        ogh = S.view(r0 + 118 * KB, BF16, [128, NSLOT * T])
        rs = S.view(r0 + 122 * KB, F32, [128, 512])
        of = S.view(r0 + 124 * KB, F32, [128, 512])
        csq = S.view(r0 + 126 * KB, F32, [128, 2, 512])
        qraw = S.view(r0 + 130 * KB, F32, [128, 512])
        assert 132 * KB <= S.nbytes - r0, (S.nbytes - r0)
        ogs = S.view(r0, BF16, [128, 64, 512])
        r_all = [S.view(r0 + 64 * KB + s * 16 * KB, F32, [128, D]) for s in range(4)]
        assert 128 * KB <= S.nbytes - r0

        init_consts(P, c, ident_d)
        P.op("dve", lambda e: e.memset(onesb.ap, 1.0), writes=[onesb])
        P.dma("sp", qg.ap, qg_d, writes=[qg])
        compute_mod(P, c, ring, wada, bada, cT_d, gate_d, c.shiftT, c.scale1T)
        gate_keys = [("dram", "gate", n) for n in range(8)]

        for slot in range(nslot):
            for s in range(4):
                stage_a(P, c, xt[slot, s * 128:(s + 1) * 128, :], 128, xbuf, ybuf,
                        lambda j, s=s: sub(hT, hT.ap[:, j, s * 128:(s + 1) * 128], j * 1024 + s * 256, j * 1024 + (s + 1) * 256),
                        s * 128, c.scale1T, c.shiftT, xkeys=[("dram", "x1", slot, s)])
            for wt in range(4):
                w = ring.load(bwin[wt], 8192)
                w3 = w.ap.rearrange("p (a b) -> p a b", b=256)
                for hh in range(2):
                    j = wt * 2 + hh
                    pz = c.ps[2 + (j % 4)]
                    for k in range(32):
                        hk = sub(hT, hT.ap[:, k, :], k * 1024, (k + 1) * 1024)
                        P.op("pe", lambda e, pz=pz, hh=hh, k=k, w3=w3, hk=hk: e.matmul(
                            pz.ap, lhsT=w3[:, k, hh * 128:(hh + 1) * 128], rhs=hk.ap, start=(k == 0), stop=(k == 31)),
                             reads=[w, hk], writes=[pz])
                    cj = sub(cqf, cqf.ap[:, j, :], j * 2048, (j + 1) * 2048)
                    P.op("act", lambda e, pz=pz, cj=cj: e.activation(out=cj.ap, in_=pz.ap, func=AF.Identity), reads=[pz], writes=[cj])
                    P.op("act", lambda e, pz=pz: e.activation(out=sqb.ap, in_=pz.ap, func=AF.Square), reads=[pz], writes=[sqb])
                    P.op("pe", lambda e, j=j: e.matmul(c.ps[6].ap, lhsT=c.onesf.ap, rhs=sqb.ap, start=(j == 0), stop=(j == 7)),
                         reads=[sqb, c.onesf], writes=[c.ps[6]])
            P.op("act", lambda e: e.activation(out=sqb.ap, in_=c.ps[6].ap, func=AF.Sqrt, bias=c.eps6.ap, scale=1.0 / 1024),
                 reads=[c.ps[6], c.eps6], writes=[sqb])
            P.op("dve", lambda e: e.reciprocal(out=rinvb.ap, in_=sqb.ap), reads=[sqb], writes=[rinvb])
            for j in range(8):
                cj = sub(cqf, cqf.ap[:, j, :], j * 2048, (j + 1) * 2048)
                dst = sub(cqnT, cqnT.ap[:, j, slot * T:(slot + 1) * T], j * 4096 + slot * 1024, j * 4096 + (slot + 1) * 1024)
                P.op("dve", lambda e, cj=cj, dst=dst, j=j: e.scalar_tensor_tensor(
                    out=dst.ap, in0=cj.ap, scalar=qg.ap[:, j:j + 1], in1=rinvb.ap, op0=ALU.mult, op1=ALU.mult),
                     reads=[cj, qg, rinvb], writes=[dst])
            for wt in range(32):
                w = ring.load(bwin[4 + wt], 8192)
                w3 = w.ap.rearrange("p (a b) -> p a b", b=256)
                for hh in range(2):
                    j = wt * 2 + hh
                    pz = c.ps[2 + (j % 4)]
                    for k in range(32):
                        hk = sub(hT, hT.ap[:, k, :], k * 1024, (k + 1) * 1024)
                        P.op("pe", lambda e, pz=pz, hh=hh, k=k, w3=w3, hk=hk: e.matmul(
                            pz.ap, lhsT=w3[:, k, hh * 128:(hh + 1) * 128], rhs=hk.ap, start=(k == 0), stop=(k == 31)),
                             reads=[w, hk], writes=[pz])
                    zb = zst[j % 2]
                    P.op("act", lambda e, pz=pz, zb=zb: e.activation(out=zb.ap, in_=pz.ap, func=AF.Silu), reads=[pz], writes=[zb])
                    P.dma("sp", sz_d[j, :, slot * T:(slot + 1) * T], zb.ap, reads=[zb], writes=[("dram", "sz", j, slot)])

        lat_keys = [("dram", "lat", st_, j_) for st_ in range(NSTEP) for j_ in range(5)]
        for kt in range(8):
            P.dma("pool", latb.ap[:, :, kt * T:(kt + 1) * T], lat_all[kt * 640:(kt + 1) * 640, :].rearrange("(j p) t -> p j t", p=128),
                  reads=lat_keys, writes=[latb])
        for m in range(4):
            mj = sub(masks, masks.ap[:, m, :], m * 1024, (m + 1) * 1024)
            P.dma("pool", mj.ap, mask_d[m], writes=[mj])
        P.dma("sp", mflag.ap, mflag_d, writes=[mflag])
        for h in range(nheads):
            if h % 8 == 0:
                wkv = ring.load(kvwb[h // 8], 8192, slot=0)
                wkv3 = wkv.ap.rearrange("p (a b) -> p a b", b=2048)
            if h % 4 == 0:
                wq = ring.load(wqb[h // 4], 8 * 768, slot=1)
                wq3 = wq.ap.rearrange("p (a b) -> p a b", b=768)
            P.dma("sp", szh.ap, sz_d[h], reads=[("dram", "sz", h, s_) for s_ in range(nslot)], writes=[szh])
            ko = (h % 8) * 256
            qo = (h % 4) * 192
            for kt in range(8):
                bank = c.ps[kt % 2]
                for kc in range(4):
                    P.op("pe", lambda e, bank=bank, kc=kc, kt=kt, ko=ko, wkv3=wkv3: e.matmul(
                        bank.ap, lhsT=wkv3[:, kc, ko:ko + 128], rhs=latb.ap[:, kc, kt * 512:(kt + 1) * 512],
                        start=(kc == 0), stop=(kc == 3)), reads=[wkv, latb], writes=[bank])
                dst = sub(kTh, kTh.ap[:, kt * 512:(kt + 1) * 512], kt * 1024, (kt + 1) * 1024)
                P.op("act", lambda e, bank=bank, dst=dst: e.activation(out=dst.ap, in_=bank.ap, func=AF.Identity), reads=[bank], writes=[dst])
            for g in range(8):
                bank = c.ps[g % 2]
                for q4 in range(4):
                    kc2 = g * 4 + q4
                    for kc in range(4):
                        P.op("pe", lambda e, bank=bank, kc=kc, kc2=kc2, q4=q4, ko=ko, wkv3=wkv3: e.matmul(
                            bank.ap[:, q4 * 128:(q4 + 1) * 128], lhsT=latb.ap[:, kc, kc2 * 128:(kc2 + 1) * 128],
                            rhs=wkv3[:, kc, ko + 128:ko + 256], start=(kc == 0), stop=(kc == 3)),
                             reads=[wkv, latb], writes=[bank])
                dst = Buf(Vh.ap[:, g * 4:(g + 1) * 4, :], list(range(Vh.res[0] + g * 2, Vh.res[0] + g * 2 + 2)))
                P.op("dve", lambda e, bank=bank, dst=dst: e.tensor_copy(out=dst.ap, in_=bank.ap.rearrange("p (a b) -> p a b", b=128)),
                     reads=[bank], writes=[dst])
            for s in range(nslot):
                bank = c.ps[2]
                for kc in range(8):
                    P.op("pe", lambda e, bank=bank, kc=kc, s=s, qo=qo, wq3=wq3: e.matmul(
                        bank.ap, lhsT=wq3[:, kc, qo:qo + 128], rhs=cqnT.ap[:, kc, s * T:(s + 1) * T], start=(kc == 0), stop=(kc == 7)),
                         reads=[wq, cqnT], writes=[bank])
                dst = sub(qn, qn.ap[:, s * T:(s + 1) * T], s * 1024, (s + 1) * 1024)
                P.op("act", lambda e, bank=bank, dst=dst: e.activation(out=dst.ap, in_=bank.ap, func=AF.Identity, scale=QSCALE),
                     reads=[bank], writes=[dst])
                bank = c.ps[3]
                for kc in range(8):
                    P.op("pe", lambda e, bank=bank, kc=kc, s=s, qo=qo, wq3=wq3: e.matmul(
                        bank.ap[0:64, :], lhsT=wq3[:, kc, qo + 128:qo + 192], rhs=cqnT.ap[:, kc, s * T:(s + 1) * T], start=(kc == 0), stop=(kc == 7)),
                         reads=[wq, cqnT], writes=[bank])
                P.dma("sp", csq.ap[0:64], csq_d[s], writes=[csq])
                cosT = csq.ap[:, 0, :]
                sinT = csq.ap[:, 1, :]
                tA, tB, tC, tD = [ropeT.ap[:, i * 512:(i + 1) * 512] for i in range(4)]
                P.op("act", lambda e, bank=bank: e.activation(out=qraw.ap[0:64], in_=bank.ap[0:64, :], func=AF.Identity, scale=QSCALE),
                     reads=[bank], writes=[qraw])
                P.op("dve", lambda e, cosT=cosT, tA=tA: e.tensor_tensor(out=tA[0:32], in0=qraw.ap[0:32], in1=cosT[0:32], op=ALU.mult),
                     reads=[qraw, csq], writes=[ropeT])
                P.op("dve", lambda e, sinT=sinT, tB=tB: e.tensor_tensor(out=tB[0:32], in0=qraw.ap[32:64], in1=sinT[32:64], op=ALU.mult),
                     reads=[qraw, csq], writes=[ropeT])
                P.op("dve", lambda e, sinT=sinT, tC=tC: e.tensor_tensor(out=tC[32:64], in0=qraw.ap[0:32], in1=sinT[0:32], op=ALU.mult),
                     reads=[qraw, csq], writes=[ropeT])
                P.op("dve", lambda e, cosT=cosT, tD=tD: e.tensor_tensor(out=tD[32:64], in0=qraw.ap[32:64], in1=cosT[32:64], op=ALU.mult),
                     reads=[qraw, csq], writes=[ropeT])
                dst = sub(qr, qr.ap[:, s * T:(s + 1) * T], s * 1024, (s + 1) * 1024)
                P.op("dve", lambda e, tA=tA, tB=tB, dst=dst: e.tensor_tensor(out=dst.ap[0:32], in0=tA[0:32], in1=tB[0:32], op=ALU.subtract),
                     reads=[ropeT], writes=[dst])
                P.op("dve", lambda e, tC=tC, tD=tD, dst=dst: e.tensor_tensor(out=dst.ap[32:64], in0=tC[32:64], in1=tD[32:64], op=ALU.add),
                     reads=[ropeT], writes=[dst])
            for s in range(nslot):
                nkc = 4 * PADLEN[s]
                po, psum_ = c.ps[4], c.ps[5]
                qs = sub(qn, qn.ap[:, s * T:(s + 1) * T], s * 1024, (s + 1) * 1024)
                qrs = sub(qr, qr.ap[0:64, s * T:(s + 1) * T], s * 1024, (s + 1) * 1024)
                LA = 2
                sbanks = [c.ps[6], c.ps[7], c.ps[3]]

                def emit_qk(kc2):
                    bank = sbanks[kc2 % 3]
                    kk = sub(kTh, kTh.ap[:, kc2 * 128:(kc2 + 1) * 128], kc2 * 256, (kc2 + 1) * 256)
                    P.op("pe", lambda e, bank=bank, kk=kk, qs=qs: e.matmul(bank.ap, lhsT=kk.ap, rhs=qs.ap, start=True, stop=False),
                         reads=[kk, qs], writes=[bank])
                    P.op("pe", lambda e, bank=bank, kc2=kc2, qrs=qrs: e.matmul(
                        bank.ap, lhsT=latb.ap[0:64, 4, kc2 * 128:(kc2 + 1) * 128], rhs=qrs.ap, start=False, stop=True),
                         reads=[latb, qrs], writes=[bank])

                for kc2 in range(min(LA, nkc)):
                    emit_qk(kc2)
                for kc2 in range(nkc):
                    if kc2 + LA < nkc:
                        emit_qk(kc2 + LA)
                    bank = sbanks[kc2 % 3]
                    pt = pT[kc2 % 2]
                    P.op("act", lambda e, bank=bank, pt=pt: e.activation(out=pt.ap, in_=bank.ap, func=AF.Exp), reads=[bank], writes=[pt])
                    mi = kc2 - (nkc - 8)
                    if mi >= 0:
                        mop = ALU.max if mi < 4 else ALU.mult
                        P.op("dve", lambda e, pt=pt, mi=mi, mop=mop, s=s: e.scalar_tensor_tensor(
                            out=pt.ap, in0=masks.ap[:, mi % 4, :], scalar=mflag.ap[:, s % 2:s % 2 + 1], in1=pt.ap, op0=mop, op1=ALU.mult),
                             reads=[pt, masks, mflag], writes=[pt])
                    vv = Buf(Vh.ap[:, kc2, :], [Vh.res[0] + kc2 // 2])
                    P.op("pe", lambda e, vv=vv, pt=pt, kc2=kc2, nkc=nkc: e.matmul(po.ap, lhsT=vv.ap, rhs=pt.ap, start=(kc2 == 0), stop=(kc2 == nkc - 1)),
                         reads=[vv, pt], writes=[po])
                    P.op("pe", lambda e, pt=pt, kc2=kc2, nkc=nkc: e.matmul(psum_.ap, lhsT=onesb.ap, rhs=pt.ap, start=(kc2 == 0), stop=(kc2 == nkc - 1)),
                         reads=[onesb, pt], writes=[psum_])
                P.op("dve", lambda e: e.reciprocal(out=rs.ap, in_=psum_.ap), reads=[psum_], writes=[rs])
                P.op("dve", lambda e: e.tensor_tensor(out=of.ap, in0=po.ap, in1=rs.ap, op=ALU.mult), reads=[po, rs], writes=[of])
                dst = sub(ogh, ogh.ap[:, s * T:(s + 1) * T], s * 1024, (s + 1) * 1024)
                P.op("dve", lambda e, dst=dst, s=s: e.tensor_tensor(out=dst.ap, in0=of.ap, in1=szh.ap[:, s * T:(s + 1) * T], op=ALU.mult),
                     reads=[of, szh], writes=[dst])
            P.dma("sp", og_d[h], ogh.ap, reads=[ogh], writes=[("dram", "og", h)])

        for slot in range(nslot):
            for hc in range(4):
                blk = Buf(ogs.ap[:, hc * 16:(hc + 1) * 16, :], list(range(ogs.res[0] + hc * 32, ogs.res[0] + (hc + 1) * 32)))
                P.dma("sp", blk.ap, og_d[hc * 16:(hc + 1) * 16, :, slot * T:(slot + 1) * T].rearrange("j p t -> p j t"),
                      reads=[("dram", "og", h_) for h_ in range(hc * 16, (hc + 1) * 16)], writes=[blk])
            for s in range(4):
                P.dma("sp", r_all[s].ap, xt[slot, s * 128:(s + 1) * 128, :], reads=[("dram", "x1", slot, s)], writes=[r_all[s]])
            outproj_residual(P, c, ring, wout, 4,
                             lambda k, s: sub(ogs, ogs.ap[:, k, s * 128:(s + 1) * 128], k * 1024 + s * 256, k * 1024 + (s + 1) * 256),
                             r_all, gate_d, gate_keys)
            final_ln(P, c, r_all, lng_d, lnb_d)
            for s in range(4):
                P.dma("sp", out_d[slot, s * 128:(s + 1) * 128, :], r_all[s].ap, reads=[r_all[s]], writes=[("dram", "out", slot, s)])
        pass


def tile_w(W, kc_per_tile, cols):
    K, N = W.shape
    g = K // (128 * kc_per_tile)
    A = W.reshape(g, kc_per_tile, 128, N // cols, cols)
    A = A.transpose(3, 0, 2, 1, 4)
    return np.ascontiguousarray(A).reshape(N // cols, g, 128, kc_per_tile * cols)


def fmaj(v):
    return np.ascontiguousarray(v.reshape(-1, 128).T)


def rep(v):
    return np.ascontiguousarray(np.broadcast_to(v[None, :], (128, v.shape[0])))


def rope_tables(pos):
    half = 32
    inv_freq = (10000.0 ** (-np.arange(half, dtype=np.float32) / half)).astype(np.float32)
    ang = pos.astype(np.float32)[:, None] * inv_freq[None, :]
    return np.cos(ang).astype(np.float32), np.sin(ang).astype(np.float32)


def l0_inputs(inp):
    x = inp["x"]
    a_w_in = inp["a_w_in"][0]
    ag = np.concatenate([a_w_in[:, 0:D].reshape(D, 32, 1, 128), a_w_in[:, D:2 * D].reshape(D, 32, 1, 128)], axis=2).reshape(D, 32 * 256)
    win_ag = tile_w(ag, 32, 256).reshape(32, 128, 8192)
    win_z = tile_w(a_w_in[:, 2 * D:], 32, 256).reshape(16, 128, 8192)
    win = np.concatenate([win_ag, win_z], axis=0)
    shared = {
        "wada": tile_w(inp["w_ada"][0], 16, 512).reshape(48, 128, 8192),
        "bada": rep(inp["b_ada"][0]),
        "win": win,
        "wdw": np.ascontiguousarray(inp["a_w_dw"][0].T.reshape(32, 128, 31).transpose(1, 0, 2)).reshape(128, 32 * 31),
        "bdw": fmaj(inp["a_b_dw"][0]),
        "cng": fmaj(inp["a_norm_g"][0]),
        "cnb": fmaj(inp["a_norm_b"][0]),
        "wout": tile_w(inp["a_w_out"][0], 16, 512).reshape(16, 128, 8192),
        "lng": rep(inp["ln_g"][0]),
        "lnb": rep(inp["ln_b"][0]),
        "kvwa": tile_w(inp["kv_w_a"], 8, 576).reshape(4, 128, 8 * 576),
        "kvg": rep(inp["kv_norm_g"]),
        "ident": np.eye(128, dtype=np.float32),
    }
    maps = []
    for core in range(8):
        b, par = core // 2, core % 2
        tiles = TILES[par] + TILES[1 - par]
        xt = np.stack([x[b, t * T:(t + 1) * T] for t in tiles])
        xh = np.stack([x[b, t * T - 32:t * T] if t > 0 else np.zeros((32, D), np.float32) for t in tiles])
        hm = np.array([1.0 if t > 0 else 0.0 for t in tiles], np.float32)
        cs = []
        for t in tiles:
            co, si = rope_tables(np.arange(t * T, (t + 1) * T))
            cs.append(np.concatenate([co, si], axis=1).reshape(4, 128, 64))
        m = dict(shared)
        idx = np.zeros((128, NSTEP * 5), np.uint32)
        for st_, t in enumerate(tiles):
            for j in range(5):
                idx[:, st_ * 5 + j] = t * 640 + j * 128 + np.arange(128)
        m.update({"xt": xt, "xh": xh, "hmask": rep(hm), "cT": fmaj(inp["c"][b]), "cs": np.stack(cs).astype(np.float32), "idx": idx})
        maps.append(m)
    return maps


def causal_masks():
    k = np.arange(128)[:, None]
    q = np.arange(512)[None, :]
    return np.stack([(q >= 128 * j + k).astype(np.float32) for j in range(4)])


def mask_flags(par):
    fl = [1.0 if (TILES[par][sl] + 1) == PADLEN[sl] else 0.0 for sl in range(2)]
    for sl in range(NSLOT):
        assert fl[sl % 2] == (1.0 if (TILES[par][sl] + 1) == PADLEN[sl] else 0.0)
        assert TILES[par][sl] + 1 in (PADLEN[sl], PADLEN[sl] - 1)
    return rep(np.array(fl, np.float32))


def l1_inputs(inp):
    b_w_in = inp["b_w_in"][0]
    shared = {
        "wada1": tile_w(inp["w_ada"][1], 16, 512).reshape(48, 128, 8192),
        "bada1": rep(inp["b_ada"][1]),
        "bwin": tile_w(b_w_in, 32, 256).reshape(36, 128, 8192),
        "qg": fmaj(inp["b_q_norm_g"][0]),
        "wqb": tile_w(inp["b_w_qb"][0], 8, 768).reshape(16, 128, 8 * 768),
        "kvwb": tile_w(inp["kv_w_b"], 4, 2048).reshape(8, 128, 4 * 2048),
        "wout1": tile_w(inp["b_w_out"][0], 16, 512).reshape(32, 128, 8192),
        "lng1": rep(inp["ln_g"][1]),
        "lnb1": rep(inp["ln_b"][1]),
        "mask": causal_masks(),
    }
    maps = []
    for core in range(8):
        par = core % 2
        csq = []
        for t in TILES[par]:
            co, si = rope_tables(np.arange(t * T, (t + 1) * T))
            cs1 = np.stack([co.T, si.T], axis=1)
            csq.append(np.concatenate([cs1, cs1], axis=0))
        m = dict(shared)
        m.update({"csq": np.stack(csq).astype(np.float32), "mflag": mask_flags(par)})
        maps.append(m)
    return maps


def build_fused():
    nc = bass.Bass("TRN2", target_bir_lowering=False, dynamic_dma_scratch_size=8192)
    t = {}
    for name, shape in list(L0_IN.items()) + list(L1_IN.items()):
        t[name] = nc.dram_tensor(name, shape, F32, kind="ExternalInput").ap()
    t["idx"] = nc.dram_tensor("idx", [128, NSTEP * 5], U32, kind="ExternalInput").ap()
    t["out"] = nc.dram_tensor("out", [NSLOT, T, D], F32, kind="ExternalOutput").ap()
    t["x1"] = nc.dram_tensor("x1_s", [NSTEP, T, D], F32, kind="Internal").ap()
    t["lat_all"] = nc.dram_tensor("lat_all", [8 * 640, T], F32, kind="Internal").ap()
    t["gate_s"] = nc.dram_tensor("gate_s", [128, D], F32, kind="Internal").ap()
    t["gate_s1"] = nc.dram_tensor("gate_s1", [128, D], F32, kind="Internal").ap()
    t["sz_s"] = nc.dram_tensor("sz_s", [64, 128, NSLOT * T], BF16, kind="Internal").ap()
    t["og_s"] = nc.dram_tensor("og_s", [64, 128, NSLOT * T], BF16, kind="Internal").ap()
    import contextlib
    with contextlib.ExitStack() as es:
        big = es.enter_context(nc.sbuf_tensor("big", [128, SB_BYTES // 4], F32))
        t["idx_sb"] = es.enter_context(nc.sbuf_tensor("idx_sb", [128, NSTEP * 5], U32))
        banks = [es.enter_context(nc.psum_tensor("ps%d" % i, [128, 512], F32)) for i in range(8)]
        P = Prog(nc)
        l0_body(nc, P, big, banks, t)
        l1_body(nc, P, big, banks, t)
        P.emit()
    return nc


def kernel(**inputs):
    inp = {k: np.asarray(v, dtype=np.float32) for k, v in inputs.items()}
    m0 = l0_inputs(inp)
    m1 = l1_inputs(inp)
    maps = []
    for core in range(8):
        m = dict(m0[core])
        m.update(m1[core])
        maps.append(m)
    nc = build_fused()
    res = run_bass_kernel_spmd(nc, maps, core_ids=list(range(8))).results
    out = np.zeros((BATCH, SEQ, D), np.float32)
    for core in range(8):
        b, par = core // 2, core % 2
        for sl, tl in enumerate(TILES[par]):
            out[b, tl * T:(tl + 1) * T] = res[core]["out"][sl]
    return out
```

```python
import numpy as np
import concourse.bass as bass
import concourse.mybir as mybir
from concourse.bass_utils import run_bass_kernel_spmd

F32 = mybir.dt.float32
BF16 = mybir.dt.bfloat16
U32 = mybir.dt.uint32
AF = mybir.ActivationFunctionType
ALU = mybir.AluOpType
AX = mybir.AxisListType

D = 4096
KD = 32
T = 512
NSLOT = 4
NO_SCATTER = False
NSTEP = 8
SEQ = 4096
BATCH = 4
ALPHA = (2.0 * 2) ** 0.25
LN_EPS = 1e-5
RMS_EPS = 1e-6
TILES = {0: [0, 3, 4, 7], 1: [1, 2, 5, 6]}
PADLEN = [2, 4, 6, 8]
NH = 64
BLK = 512


class Buf:
    __slots__ = ("ap", "res")

    def __init__(self, ap, res):
        self.ap = ap
        self.res = res

    def __getitem__(self, idx):
        return self.ap[idx]


class _Op:
    __slots__ = ("fn", "waits", "sig", "sigval", "dsem", "dtarget")

    def __init__(self, fn):
        self.fn = fn
        self.waits = []
        self.sig = False
        self.sigval = 0
        self.dsem = None
        self.dtarget = 0


class _Res:
    __slots__ = ("w", "r")

    def __init__(self):
        self.w = None
        self.r = {}


ENGS = ("pe", "dve", "act", "pool", "sp")
NDSEM = 12


class Prog:
    def __init__(self, nc):
        self.nc = nc
        self.ops = {e: [] for e in ENGS}
        self.res = {}
        self.waited = {e: {} for e in ENGS}
        self.dcount = {}
        self.drr = {e: 0 for e in ENGS}

    def _get(self, key):
        st = self.res.get(key)
        if st is None:
            st = _Res()
            self.res[key] = st
        return st

    @staticmethod
    def _keys(items):
        out = []
        for it in items:
            if isinstance(it, Buf):
                out.extend(it.res)
            elif isinstance(it, (list, tuple)) and it and isinstance(it[0], (Buf, list)):
                out.extend(Prog._keys(it))
            else:
                out.append(it)
        return out

    def _resolve(self, eng, o, rkeys, wkeys, is_dma):
        need = {}
        for k in rkeys:
            st = self._get(k)
            if st.w is not None:
                self._need(need, st.w)
        for k in wkeys:
            st = self._get(k)
            if st.w is not None:
                self._need(need, st.w)
            for ev in st.r.values():
                self._need(need, ev)
        wd = self.waited[eng]
        for src, ev in need.items():
            if ev[0] == "e":
                if ev[1] == eng and not is_dma:
                    continue
                if wd.get(src, -1) >= ev[2]:
                    continue
                wd[src] = ev[2]
                o.waits.append(ev)
                self.ops[ev[1]][ev[2]].sig = True
            else:
                if wd.get(src, -1) >= ev[2]:
                    continue
                wd[src] = ev[2]
                o.waits.append(ev)

    @staticmethod
    def _need(need, ev):
        src = ev[1]
        cur = need.get(src)
        if cur is None or cur[2] < ev[2]:
            need[src] = ev

    def _record(self, me, rkeys, wkeys):
        for k in rkeys:
            self._get(k).r[me[1]] = me
        for k in wkeys:
            st = self._get(k)
            st.w = me
            st.r = {}

    def op(self, eng, fn, reads=(), writes=(), strict=False):
        rkeys = self._keys(reads)
        wkeys = self._keys(writes)
        o = _Op(fn)
        self._resolve(eng, o, rkeys, wkeys, strict)
        me = ("e", eng, len(self.ops[eng]))
        self.ops[eng].append(o)
        self._record(me, rkeys, wkeys)

    def dma(self, q, out, in_, reads=(), writes=(), fn=None):
        rkeys = self._keys(reads)
        wkeys = self._keys(writes)
        o = _Op(fn if fn is not None else (lambda e: e.dma_start(out=out, in_=in_)))
        self._resolve(q, o, rkeys, wkeys, True)
        k = self.drr[q]
        self.drr[q] = (k + 1) % NDSEM
        src = ("d", q, k)
        prev = self.dcount.get(src, 0)
        if prev and self.waited[q].get(src, -1) < prev:
            self.waited[q][src] = prev
            o.waits.append(("d", src, prev))
        tgt = prev + 16
        self.dcount[src] = tgt
        o.dsem = src
        o.dtarget = tgt
        self.ops[q].append(o)
        self._record(("d", src, tgt), rkeys, wkeys)

    def emit(self):
        nc = self.nc
        import contextlib
        with contextlib.ExitStack() as es:
            esem = {e: es.enter_context(nc.semaphore("s_" + e)) for e in ENGS}
            dsem = {}
            for q in ("sp", "pool", "act"):
                for k in range(NDSEM):
                    dsem[("d", q, k)] = es.enter_context(nc.semaphore("d_%s%d" % (q, k)))
            for e in ENGS:
                c = 0
                for o in self.ops[e]:
                    if o.sig:
                        c += 1
                        o.sigval = c
            fin = _Op(None)
            for src, tgt in self.dcount.items():
                fin.waits.append(("d", src, tgt))
            block = es.enter_context(nc.Block())
            ops = self.ops

            def run(eng_name, eng):
                for o in ops[eng_name]:
                    for ev in o.waits:
                        if ev[0] == "e":
                            eng.wait_ge(esem[ev[1]], ops[ev[1]][ev[2]].sigval)
                        else:
                            eng.wait_ge(dsem[ev[1]], ev[2])
                    ins = o.fn(eng)
                    if o.dsem is not None:
                        ins.then_inc(dsem[o.dsem], 16)
                    elif o.sig:
                        ins.then_inc(esem[eng_name], 1)

            @block.tensor
            def _(t):
                run("pe", t)

            @block.vector
            def _(v):
                run("dve", v)

            @block.scalar
            def _(a):
                run("act", a)

            @block.gpsimd
            def _(g):
                run("pool", g)

            @block.sync
            def _(s):
                run("sp", s)
                for ev in fin.waits:
                    s.wait_ge(dsem[ev[1]], ev[2])


class Sbuf:
    def __init__(self, big, nbytes):
        self.big = big
        self.nbytes = nbytes
        self.top = 0

    def alloc(self, nbytes):
        nbytes = (nbytes + BLK - 1) // BLK * BLK
        off = self.top
        self.top += nbytes
        assert self.top <= self.nbytes, ("sbuf overflow", self.top, self.nbytes)
        return off

    def view(self, off, dtype, shape):
        esz = 4 if dtype == F32 else 2
        n = 1
        for s in shape[1:]:
            n *= s
        nb = n * esz
        assert off % 4 == 0 and nb % 4 == 0
        ap = self.big[:, off // 4:(off + nb) // 4]
        if dtype != F32:
            ap = ap.bitcast(dtype)
        if len(shape) == 3:
            ap = ap.rearrange("p (a b) -> p a b", b=shape[2])
        if shape[0] != 128:
            ap = ap[0:shape[0]]
        res = list(range(off // BLK, (off + nb + BLK - 1) // BLK))
        return Buf(ap, res)

    def new(self, dtype, shape):
        esz = 4 if dtype == F32 else 2
        n = 1
        for s in shape[1:]:
            n *= s
        return self.view(self.alloc(n * esz), dtype, shape)


def sub(buf, ap, lo_bytes=None, hi_bytes=None):
    if lo_bytes is None:
        return Buf(ap, buf.res)
    b0 = buf.res[0]
    return Buf(ap, list(range(b0 + lo_bytes // BLK, b0 + (hi_bytes + BLK - 1) // BLK)))


class Ctx:
    pass


def setup_common(nc, P, S, c):
    c.ps = []
    c.nc = nc
    c.P = P
    c.S = S


def ln_stats(P, c, xs, ntok, mv, st, sd, rstd, nmr, eps_t):
    for i in range(8):
        P.op("dve", lambda e, i=i: e.bn_stats(out=st[0:ntok, i * 6:(i + 1) * 6], in_=xs[0:ntok, i * 512:(i + 1) * 512]),
             reads=[xs], writes=[st])
    P.op("dve", lambda e: e.bn_aggr(out=mv[0:ntok, :], in_=st[0:ntok, :]), reads=[st], writes=[mv], strict=True)
    P.op("act", lambda e: e.activation(out=sd[0:ntok, :], in_=mv[0:ntok, 1:2], func=AF.Sqrt, bias=eps_t[0:ntok, :], scale=1.0),
         reads=[mv, eps_t], writes=[sd])
    P.op("dve", lambda e: e.reciprocal(out=rstd[0:ntok, :], in_=sd[0:ntok, :]), reads=[sd], writes=[rstd])
    P.op("dve", lambda e: e.scalar_tensor_tensor(out=nmr[0:ntok, :], in0=mv[0:ntok, 0:1], scalar=-1.0, in1=rstd[0:ntok, :],
                                                 op0=ALU.mult, op1=ALU.mult), reads=[mv, rstd], writes=[nmr], strict=True)


def stage_a(P, c, x_dram_ap, ntok, xbuf, ybuf, hT_dst, tok0, scale1T, shiftT, xkeys=()):
    P.dma("sp", xbuf[0:ntok, :], x_dram_ap, reads=list(xkeys), writes=[xbuf])
    ln_stats(P, c, xbuf, ntok, c.mv, c.st, c.sd, c.rstd, c.nmr, c.eps5)
    P.op("dve", lambda e: e.tensor_scalar(out=ybuf[0:ntok, :], in0=xbuf[0:ntok, :], scalar1=c.rstd[0:ntok, :],
                                          scalar2=c.nmr[0:ntok, :], op0=ALU.mult, op1=ALU.add),
         reads=[xbuf, c.rstd, c.nmr], writes=[ybuf], strict=True)
    for grp in range(8):
        bank = c.ps[grp % 2]
        psb = bank.ap[:, 0:256].bitcast(BF16)
        for q in range(4):
            j = grp * 4 + q
            P.op("pe", lambda e, j=j, q=q, psb=psb: e.transpose(out=psb[:, q * 128:q * 128 + ntok],
                                                                  in_=ybuf[0:ntok, j * 128:(j + 1) * 128],
                                                                  identity=c.identb[0:ntok, 0:ntok]),
                 reads=[ybuf, c.identb], writes=[bank])
        for q in range(4):
            j = grp * 4 + q
            dst = hT_dst(j)
            P.op("act", lambda e, j=j, q=q, psb=psb, dst=dst: e.activation(
                out=dst.ap, in_=psb[:, q * 128:q * 128 + ntok], func=AF.Identity,
                scale=scale1T[:, j:j + 1], bias=shiftT[:, j:j + 1]),
                 reads=[bank, scale1T, shiftT], writes=[dst])


class WRing:
    def __init__(self, P, S, nslots, slot_elems):
        self.P = P
        self.slots = [S.new(BF16, [128, slot_elems]) for _ in range(nslots)]
        self.i = 0
        self.slot_elems = slot_elems

    def load(self, dram_ap, nelem, slot=None):
        if slot is None:
            s = self.slots[self.i % len(self.slots)]
            self.i += 1
        else:
            s = self.slots[slot]
        dst = Buf(s.ap[:, 0:nelem], s.res)
        b = 2048
        while nelem % b:
            b -= 128
        if nelem > b:
            o3 = dst.ap.rearrange("p (a b) -> p a b", b=b)
            i3 = dram_ap.rearrange("p (a b) -> p a b", b=b)
        else:
            o3, i3 = dst.ap, dram_ap
        self.P.dma("pool", o3, i3, reads=[], writes=[dst])
        return dst


def compute_mod(P, c, ring, wada, bada, cT_d, gate_d, shiftT, scale1T):
    S = c.S
    cT = c.tmpA
    P.dma("sp", cT.ap, cT_d, writes=[cT])
    scb = c.scb
    P.op("act", lambda e: e.activation(out=scb.ap, in_=cT.ap.unsqueeze(2).broadcast_to([128, 32, 128]), func=AF.Silu),
         reads=[cT], writes=[scb])
    for n in range(24):
        bank = c.ps[n % 8]
        bp = c.piece[n % 2]
        P.dma("sp", bp.ap, bada[:, n * 512:(n + 1) * 512], writes=[bp])
        for kg in range(2):
            w = ring.load(wada[n * 2 + kg], 16 * 512)
            w3 = w.ap.rearrange("p (a b) -> p a b", b=512)
            for kc in range(16):
                k = kg * 16 + kc
                P.op("pe", lambda e, k=k, kc=kc, w3=w3, bank=bank: e.matmul(
                    bank.ap, lhsT=scb.ap[:, k, :], rhs=w3[:, kc, :], start=(k == 0), stop=(k == 31)),
                     reads=[scb, w], writes=[bank])
        mp = c.modp[n % 2]
        if n < 16:
            P.op("dve", lambda e, bank=bank, bp=bp, mp=mp: e.tensor_tensor(out=mp.ap, in0=bank.ap, in1=bp.ap, op=ALU.add),
                 reads=[bank, bp], writes=[mp])
            dg = c.dg
            P.op("dve", lambda e, mp=mp, dg=dg: e.tensor_tensor(
                out=dg.ap, in0=mp.ap.rearrange("p (a b) -> p a b", b=128),
                in1=c.identf.ap.unsqueeze(1).broadcast_to([128, 4, 128]), op=ALU.mult),
                 reads=[mp, c.identf], writes=[dg])
            dst = shiftT if n < 8 else scale1T
            col = (n % 8) * 4
            P.op("dve", lambda e, dg=dg, dst=dst, col=col: e.tensor_reduce(
                out=dst.ap[:, col:col + 4], in_=dg.ap, op=ALU.add, axis=AX.X),
                 reads=[dg], writes=[dst])
        else:
            P.op("dve", lambda e, bank=bank, bp=bp, mp=mp: e.scalar_tensor_tensor(
                out=mp.ap, in0=bank.ap, scalar=1.0, in1=bp.ap, op0=ALU.add, op1=ALU.add),
                 reads=[bank, bp], writes=[mp])
            col = (n - 16) * 512
            P.dma("sp", gate_d[:, col:col + 512], mp.ap, reads=[mp], writes=[("dram", "gate", n - 16)])
    P.op("dve", lambda e: e.tensor_scalar(out=scale1T.ap, in0=scale1T.ap, scalar1=1.0, scalar2=None, op0=ALU.add),
         reads=[scale1T], writes=[scale1T], strict=True)


def final_ln(P, c, r_all, g_d, b_d, nsub=4):
    for s in range(nsub):
        r = r_all[s]
        ln_stats(P, c, r, 128, c.mv4[s], c.st, c.sd, c.rstd4[s], c.nmr4[s], c.eps5)
        P.op("act", lambda e, r=r, s=s: e.activation(out=r.ap, in_=r.ap, func=AF.Identity,
                                                   scale=c.rstd4[s].ap, bias=c.nmr4[s].ap),
             reads=[r, c.rstd4[s], c.nmr4[s]], writes=[r])
    for n in range(8):
        gp = (c.piece[0], c.modp[0])[n % 2]
        bp = (c.piece[1], c.modp[1])[n % 2]
        P.dma("sp", gp.ap, g_d[:, n * 512:(n + 1) * 512], writes=[gp])
        P.dma("sp", bp.ap, b_d[:, n * 512:(n + 1) * 512], writes=[bp])
        for s in range(nsub):
            rs = sub(r_all[s], r_all[s].ap[:, n * 512:(n + 1) * 512], n * 2048, (n + 1) * 2048)
            P.op("dve", lambda e, rs=rs, gp=gp: e.tensor_tensor(out=rs.ap, in0=rs.ap, in1=gp.ap, op=ALU.mult),
                 reads=[rs, gp], writes=[rs])
            P.op("dve", lambda e, rs=rs, bp=bp: e.tensor_tensor(out=rs.ap, in0=rs.ap, in1=bp.ap, op=ALU.add),
                 reads=[rs, bp], writes=[rs])


def outproj_residual(P, c, ring, wtiles, nkg, lhs_chunk, r_all, gate_d, gate_keys):
    for n in range(8):
        gp = c.piece[n % 2]
        P.dma("sp", gp.ap, gate_d[:, n * 512:(n + 1) * 512], reads=[gate_keys[n]], writes=[gp])
        banks = [c.ps[(n % 2) * 4 + s] for s in range(4)]
        for kg in range(nkg):
            w = ring.load(wtiles[n * nkg + kg], 16 * 512)
            w3 = w.ap.rearrange("p (a b) -> p a b", b=512)
            for s in range(4):
                for kc in range(16):
                    k = kg * 16 + kc
                    lh = lhs_chunk(k, s)
                    P.op("pe", lambda e, lh=lh, kc=kc, w3=w3, bank=banks[s], k=k: e.matmul(
                        bank.ap, lhsT=lh.ap, rhs=w3[:, kc, :], start=(k == 0), stop=(k == nkg * 16 - 1)),
                         reads=[lh, w], writes=[banks[s]])
        for s in range(4):
            tmp = c.modp[s % 2]
            rs = sub(r_all[s], r_all[s].ap[:, n * 512:(n + 1) * 512], n * 2048, (n + 1) * 2048)
            P.op("dve", lambda e, bank=banks[s], gp=gp, tmp=tmp: e.tensor_tensor(out=tmp.ap, in0=bank.ap, in1=gp.ap, op=ALU.mult),
                 reads=[banks[s], gp], writes=[tmp])
            P.op("dve", lambda e, rs=rs, tmp=tmp: e.scalar_tensor_tensor(out=rs.ap, in0=rs.ap, scalar=ALPHA, in1=tmp.ap,
                                                                       op0=ALU.mult, op1=ALU.add),
                 reads=[rs, tmp], writes=[rs])


def alloc_small(c, S):
    a = S.alloc(512)
    c.mv = S.view(a, F32, [128, 2])
    c.sd = S.view(a + 8, F32, [128, 1])
    c.rstd = S.view(a + 12, F32, [128, 1])
    c.nmr = S.view(a + 16, F32, [128, 1])
    c.st = S.view(a + 64, F32, [128, 48])
    b = S.alloc(512)
    c.mv4 = [S.view(b + 8 * i, F32, [128, 2]) for i in range(4)]
    c.rstd4 = [S.view(b + 32 + 4 * i, F32, [128, 1]) for i in range(4)]
    c.nmr4 = [S.view(b + 48 + 4 * i, F32, [128, 1]) for i in range(4)]
    k = S.alloc(512)
    c.eps5 = S.view(k, F32, [128, 1])
    c.eps6 = S.view(k + 4, F32, [128, 1])
    c.identb = S.new(BF16, [128, 128])
    c.identf = S.new(F32, [128, 128])
    c.onesf = S.new(F32, [128, 128])
    m = S.alloc(512)
    c.shiftT = S.view(m, F32, [128, 32])
    c.scale1T = S.view(m + 128, F32, [128, 32])
    c.tmpA = S.view(m + 256, F32, [128, 32])
    c.dg = S.new(F32, [128, 4, 128])
    c.piece = [S.new(F32, [128, 512]) for _ in range(2)]
    c.modp = [S.new(F32, [128, 512]) for _ in range(2)]


def init_consts(P, c, ident_d):
    P.dma("sp", c.identf.ap, ident_d, writes=[c.identf])
    P.dma("pool", c.identb.ap, ident_d, writes=[c.identb])
    P.op("dve", lambda e: e.memset(c.eps5.ap, LN_EPS), writes=[c.eps5])
    P.op("dve", lambda e: e.memset(c.eps6.ap, RMS_EPS), writes=[c.eps6])
    P.op("dve", lambda e: e.memset(c.onesf.ap, 1.0), writes=[c.onesf])


L0_IN = {"xt": [NSTEP, T, D], "xh": [NSTEP, 32, D], "hmask": [128, NSTEP], "cT": [128, 32], "wada": [48, 128, 8192],
         "bada": [128, 3 * D], "win": [48, 128, 8192], "wdw": [128, 32 * 31], "bdw": [128, 32], "cng": [128, 32],
         "cnb": [128, 32], "wout": [16, 128, 8192], "lng": [128, D], "lnb": [128, D], "kvwa": [4, 128, 8 * 576],
         "kvg": [128, 512], "cs": [NSTEP, 4, 128, 64], "ident": [128, 128]}
SB_BYTES = 183 * 1024


def l0_body(nc, P, big, banks, t, nslot=NSTEP):
    xt, xh, hmask_d, cT_d, wada, bada, win = t["xt"], t["xh"], t["hmask"], t["cT"], t["wada"], t["bada"], t["win"]
    wdw_d, bdw_d, cng_d, cnb_d, wout, lng_d, lnb_d = t["wdw"], t["bdw"], t["cng"], t["cnb"], t["wout"], t["lng"], t["lnb"]
    kvwa, kvg_d, cs_d, ident_d = t["kvwa"], t["kvg"], t["cs"], t["ident"]
    x1, lat_all, idx_d, gate_d = t["x1"], t["lat_all"], t["idx"], t["gate_s"]
    if True:
        S = Sbuf(big, SB_BYTES)
        c = Ctx()
        setup_common(nc, P, S, c)
        c.ps = [Buf(b[:], [("ps", i)]) for i, b in enumerate(banks)]
        alloc_small(c, S)
        wdw = S.new(F32, [128, 32 * 31])
        kk = S.alloc(512)
        bdw = S.view(kk, F32, [128, 32])
        cng = S.view(kk + 128, F32, [128, 32])
        cnb = S.view(kk + 256, F32, [128, 32])
        hmask = S.view(kk + 384, F32, [128, NSTEP])
        idx_sb = Buf(t["idx_sb"][:], [("sb", "idx")])
        kvg = S.new(F32, [128, 512])
        meanb = S.new(F32, [128, 512])
        rstdb = S.new(F32, [128, 512])
        ring = WRing(P, S, 2, 8192)
        cvog_off = S.alloc(32 * 1024)
        cvog = S.view(cvog_off, BF16, [128, 32, 512])
        r1 = S.alloc(64 * 1024)
        hT = S.view(r1, BF16, [128, 32, 512])
        xbuf = S.view(r1 + 32 * 1024, F32, [128, D])
        ybuf = S.view(r1 + 48 * 1024, BF16, [128, D])
        hTh = S.view(r1 + 56 * 1024, BF16, [128, 32, 32])
        c.scb = S.view(r1, BF16, [128, 32, 128])
        r_all = [S.view(r1 + s * 16 * 1024, F32, [128, D]) for s in range(4)]
        r2 = S.alloc(28 * 1024)
        o = r2
        sig = []; vb = []; acc = []; sq = []
        for i in range(2):
            sig.append(S.view(o, F32, [128, 544])); o += 544 * 4
            vb.append(S.view(o, F32, [128, 544])); o += 544 * 4
            acc.append(S.view(o, F32, [128, 512])); o += 2048
            sq.append(S.view(o, F32, [128, 512])); o += 2048
        assert o - r2 <= 18 * 1024
        sz = [S.view(r2 + i * 6144, F32, [128, 512]) for i in range(2)]
        n1 = [S.view(r2 + i * 6144 + 2048, F32, [128, 512]) for i in range(2)]
        sact = [S.view(r2 + i * 6144 + 4096, F32, [128, 512]) for i in range(2)]
        x1b = S.view(r2, BF16, [128, D])
        lat_tok = S.view(r2 + 8192, F32, [128, 640])
        latTs = S.view(r2 + 18 * 1024, F32, [128, 5, 512])
        junk = S.view(r2 + 8192 + 5120, F32, [128, 512])
        ropet = S.view(r2 + 8192 + 5120 + 2048, F32, [128, 4 * 32])
        cs = S.view(r2 + 8192 + 5120 + 2048 + 512, F32, [128, 4 * 64])
        ss = S.view(r2 + 8192 + 5120 + 2048 + 512 + 1024, F32, [128, 1])
        rinv = S.view(r2 + 8192 + 5120 + 2048 + 512 + 1024 + 512, F32, [128, 1])
        x1T = cvog
        print("L0 sbuf bytes used", S.top)

        init_consts(P, c, ident_d)
        P.dma("sp", wdw.ap, wdw_d, writes=[wdw])
        P.dma("sp", bdw.ap, bdw_d, writes=[bdw])
        P.dma("sp", cng.ap, cng_d, writes=[cng])
        P.dma("sp", cnb.ap, cnb_d, writes=[cnb])
        P.dma("sp", hmask.ap, hmask_d, writes=[hmask])
        P.dma("sp", idx_sb.ap, idx_d, writes=[idx_sb])
        P.dma("sp", kvg.ap, kvg_d, writes=[kvg])

        compute_mod(P, c, ring, wada, bada, cT_d, gate_d, c.shiftT, c.scale1T)
        gate_keys = [("dram", "gate", n) for n in range(8)]

        for slot in range(nslot):
            for s in range(4):
                stage_a(P, c, xt[slot, s * 128:(s + 1) * 128, :], 128, xbuf, ybuf,
                        lambda j, s=s: sub(hT, hT.ap[:, j, s * 128:(s + 1) * 128], j * 1024 + s * 256, j * 1024 + (s + 1) * 256),
                        s * 128, c.scale1T, c.shiftT)
            stage_a(P, c, xh[slot], 32, xbuf, ybuf,
                    lambda j: sub(hTh, hTh.ap[:, j, :], j * 64, (j + 1) * 64), 0, c.scale1T, c.shiftT)
            pending_stats = []
            for ch in range(32):
                w = ring.load(win[ch], 8192)
                w3 = w.ap.rearrange("p (a b) -> p a b", b=256)
                pa = c.ps[2 + 2 * (ch % 2)]
                pg = c.ps[3 + 2 * (ch % 2)]
                ph = c.ps[ch % 2]
                for (bank, coff) in ((pg, 128), (pa, 0)):
                    for k in range(32):
                        hk = sub(hT, hT.ap[:, k, :], k * 1024, (k + 1) * 1024)
                        P.op("pe", lambda e, bank=bank, coff=coff, k=k, w3=w3, hk=hk: e.matmul(
                            bank.ap, lhsT=w3[:, k, coff:coff + 128], rhs=hk.ap, start=(k == 0), stop=(k == 31)),
                             reads=[w, hk], writes=[bank])
                for (col0, coff) in ((0, 0), (32, 128)):
                    for k in range(32):
                        hk = sub(hTh, hTh.ap[:, k, :], k * 64, (k + 1) * 64)
                        P.op("pe", lambda e, col0=col0, coff=coff, k=k, w3=w3, hk=hk, ph=ph: e.matmul(
                            ph.ap[:, col0:col0 + 32], lhsT=w3[:, k, coff:coff + 128], rhs=hk.ap, start=(k == 0), stop=(k == 31)),
                             reads=[w, hk], writes=[ph])
                for fn_ in pending_stats:
                    fn_()
                pending_stats = []
                i2 = ch % 2
                sg, v, ac, sqq = sig[i2], vb[i2], acc[i2], sq[i2]
                P.op("act", lambda e, sg=sg, pg=pg: e.activation(out=sg.ap[:, 32:544], in_=pg.ap, func=AF.Sigmoid),
                     reads=[pg], writes=[sg])
                P.op("act", lambda e, sg=sg, ph=ph: e.activation(out=sg.ap[:, 0:32], in_=ph.ap[:, 32:64], func=AF.Sigmoid),
                     reads=[ph], writes=[sg])
                P.op("dve", lambda e, v=v, pa=pa, sg=sg: e.tensor_tensor(out=v.ap[:, 32:544], in0=pa.ap, in1=sg.ap[:, 32:544], op=ALU.mult),
                     reads=[pa, sg], writes=[v])
                P.op("dve", lambda e, v=v, ph=ph, sg=sg, slot=slot: e.scalar_tensor_tensor(
                    out=v.ap[:, 0:32], in0=ph.ap[:, 0:32], scalar=hmask.ap[:, slot:slot + 1], in1=sg.ap[:, 0:32],
                    op0=ALU.mult, op1=ALU.mult), reads=[ph, sg, hmask], writes=[v])
                P.op("dve", lambda e, v=v, ac=ac, ch=ch: e.tensor_scalar(
                    out=ac.ap, in0=v.ap[:, 2:514], scalar1=wdw.ap[:, ch * 31:ch * 31 + 1], scalar2=bdw.ap[:, ch:ch + 1],
                    op0=ALU.mult, op1=ALU.add), reads=[v, wdw, bdw], writes=[ac])
                for k in range(1, 31):
                    P.op("dve", lambda e, v=v, ac=ac, ch=ch, k=k: e.scalar_tensor_tensor(
                        out=ac.ap, in0=v.ap[:, 2 + k:514 + k], scalar=wdw.ap[:, ch * 31 + k:ch * 31 + k + 1], in1=ac.ap,
                        op0=ALU.mult, op1=ALU.add), reads=[v, ac], writes=[ac])
                P.op("act", lambda e, ac=ac, sqq=sqq: e.activation(out=sqq.ap, in_=ac.ap, func=AF.Square),
                     reads=[ac], writes=[sqq])
                cv = sub(cvog, cvog.ap[:, ch, :], ch * 1024, (ch + 1) * 1024)
                P.op("act", lambda e, ac=ac, cv=cv: e.activation(out=cv.ap, in_=ac.ap, func=AF.Identity),
                     reads=[ac], writes=[cv])
                pending_stats.append(lambda ac=ac, ch=ch: P.op(
                    "pe", lambda e: e.matmul(c.ps[6].ap, lhsT=c.onesf.ap, rhs=ac.ap, start=(ch == 0), stop=(ch == 31)),
                    reads=[ac, c.onesf], writes=[c.ps[6]]))
                pending_stats.append(lambda sqq=sqq, ch=ch: P.op(
                    "pe", lambda e: e.matmul(c.ps[7].ap, lhsT=c.onesf.ap, rhs=sqq.ap, start=(ch == 0), stop=(ch == 31)),
                    reads=[sqq, c.onesf], writes=[c.ps[7]]))
            for fn_ in pending_stats:
                fn_()
            pending_stats = []
            P.op("dve", lambda e: e.tensor_scalar(out=meanb.ap, in0=c.ps[6].ap, scalar1=1.0 / D, scalar2=None, op0=ALU.mult),
                 reads=[c.ps[6]], writes=[meanb])
            P.op("dve", lambda e: e.tensor_tensor(out=c.modp[0].ap, in0=meanb.ap, in1=meanb.ap, op=ALU.mult),
                 reads=[meanb], writes=[c.modp[0]])
            P.op("dve", lambda e: e.scalar_tensor_tensor(out=c.modp[1].ap, in0=c.ps[7].ap, scalar=1.0 / D, in1=c.modp[0].ap,
                                                         op0=ALU.mult, op1=ALU.subtract),
                 reads=[c.ps[7], c.modp[0]], writes=[c.modp[1]])
            P.op("act", lambda e: e.activation(out=c.modp[0].ap, in_=c.modp[1].ap, func=AF.Sqrt, bias=c.eps5.ap, scale=1.0),
                 reads=[c.modp[1], c.eps5], writes=[c.modp[0]])
            P.op("dve", lambda e: e.reciprocal(out=rstdb.ap, in_=c.modp[0].ap), reads=[c.modp[0]], writes=[rstdb])
            for zt in range(16):
                w = ring.load(win[32 + zt], 8192)
                w3 = w.ap.rearrange("p (a b) -> p a b", b=256)
                for hh in range(2):
                    ch = zt * 2 + hh
                    pz = c.ps[2 + (ch % 4)]
                    for k in range(32):
                        hk = sub(hT, hT.ap[:, k, :], k * 1024, (k + 1) * 1024)
                        P.op("pe", lambda e, pz=pz, hh=hh, k=k, w3=w3, hk=hk: e.matmul(
                            pz.ap, lhsT=w3[:, k, hh * 128:(hh + 1) * 128], rhs=hk.ap, start=(k == 0), stop=(k == 31)),
                             reads=[w, hk], writes=[pz])
                    i2 = ch % 2
                    cv = sub(cvog, cvog.ap[:, ch, :], ch * 1024, (ch + 1) * 1024)
                    P.op("act", lambda e, pz=pz, i2=i2: e.activation(out=sz[i2].ap, in_=pz.ap, func=AF.Silu),
                         reads=[pz], writes=[sz[i2]])
                    P.op("dve", lambda e, cv=cv, i2=i2: e.tensor_tensor(out=n1[i2].ap, in0=cv.ap, in1=meanb.ap, op=ALU.subtract),
                         reads=[cv, meanb], writes=[n1[i2]])
                    P.op("dve", lambda e, i2=i2: e.tensor_tensor(out=n1[i2].ap, in0=n1[i2].ap, in1=rstdb.ap, op=ALU.mult),
                         reads=[n1[i2], rstdb], writes=[n1[i2]])
                    P.op("act", lambda e, i2=i2, ch=ch: e.activation(out=sact[i2].ap, in_=n1[i2].ap, func=AF.Silu,
                                                                   scale=cng.ap[:, ch:ch + 1], bias=cnb.ap[:, ch:ch + 1]),
                         reads=[n1[i2], cng, cnb], writes=[sact[i2]])
                    P.op("dve", lambda e, cv=cv, i2=i2: e.tensor_tensor(out=cv.ap, in0=sact[i2].ap, in1=sz[i2].ap, op=ALU.mult),
                         reads=[sact[i2], sz[i2]], writes=[cv])
            for s in range(4):
                P.dma("sp", r_all[s].ap, xt[slot, s * 128:(s + 1) * 128, :], writes=[r_all[s]])
            outproj_residual(P, c, ring, wout, 2,
                             lambda k, s: sub(cvog, cvog.ap[:, k, s * 128:(s + 1) * 128], k * 1024 + s * 256, k * 1024 + (s + 1) * 256),
                             r_all, gate_d, gate_keys)
            final_ln(P, c, r_all, lng_d, lnb_d)
            for s in range(4):
                P.dma("sp", x1[slot, s * 128:(s + 1) * 128, :], r_all[s].ap, reads=[r_all[s]], writes=[("dram", "x1", slot, s)])
            P.dma("sp", cs.ap.rearrange("p (a b) -> p a b", b=64), cs_d[slot].rearrange("a p b -> p a b"), writes=[cs])
            for s in range(4):
                P.op("act", lambda e, s=s: e.activation(out=x1b.ap, in_=r_all[s].ap, func=AF.Identity), reads=[r_all[s]], writes=[x1b])
                for grp in range(8):
                    bank = c.ps[grp % 2]
                    psb = bank.ap[:, 0:256].bitcast(BF16)
                    for q in range(4):
                        j = grp * 4 + q
                        P.op("pe", lambda e, j=j, q=q, psb=psb: e.transpose(out=psb[:, q * 128:(q + 1) * 128],
                                                                              in_=x1b.ap[:, j * 128:(j + 1) * 128],
                                                                              identity=c.identb.ap),
                             reads=[x1b, c.identb], writes=[bank])
                    dst = Buf(x1T.ap[:, grp * 4:grp * 4 + 4, s * 128:(s + 1) * 128],
                              list(range(x1T.res[0] + grp * 8, x1T.res[0] + grp * 8 + 8)))
                    P.op("dve", lambda e, psb=psb, dst=dst: e.tensor_copy(out=dst.ap, in_=psb.rearrange("p (a b) -> p a b", b=128)),
                         reads=[bank], writes=[dst])
            for kg in range(4):
                w = ring.load(kvwa[kg], 8 * 576)
                w3 = w.ap.rearrange("p (a b) -> p a b", b=576)
                for s in range(4):
                    for kc in range(8):
                        k = kg * 8 + kc
                        lh = sub(x1T, x1T.ap[:, k, s * 128:(s + 1) * 128], k * 1024 + s * 256, k * 1024 + (s + 1) * 256)
                        P.op("pe", lambda e, s=s, k=k, kc=kc, w3=w3, lh=lh: e.matmul(
                            c.ps[s].ap, lhsT=lh.ap, rhs=w3[:, kc, 0:512], start=(k == 0), stop=(k == 31)),
                             reads=[lh, w], writes=[c.ps[s]])
                        P.op("pe", lambda e, s=s, k=k, kc=kc, w3=w3, lh=lh: e.matmul(
                            c.ps[4 + s].ap[:, 0:64], lhsT=lh.ap, rhs=w3[:, kc, 512:576], start=(k == 0), stop=(k == 31)),
                             reads=[lh, w], writes=[c.ps[4 + s]])
            for s in range(4):
                pa, pb = c.ps[s], c.ps[4 + s]
                P.op("dve", lambda e: e.memset(lat_tok.ap[:, 576:640], 0.0), writes=[lat_tok])
                P.op("act", lambda e, pa=pa: e.activation(out=junk.ap, in_=pa.ap, func=AF.Square, accum_out=ss.ap),
                     reads=[pa], writes=[junk, ss])
                P.op("act", lambda e: e.activation(out=ss.ap, in_=ss.ap, func=AF.Sqrt, bias=c.eps6.ap, scale=1.0 / 512),
                     reads=[ss, c.eps6], writes=[ss], strict=True)
                P.op("dve", lambda e: e.reciprocal(out=rinv.ap, in_=ss.ap), reads=[ss], writes=[rinv])
                P.op("dve", lambda e, pa=pa: e.scalar_tensor_tensor(out=lat_tok.ap[:, 0:512], in0=pa.ap, scalar=rinv.ap[:, 0:1],
                                                                  in1=kvg.ap, op0=ALU.mult, op1=ALU.mult),
                     reads=[pa, rinv, kvg], writes=[lat_tok], strict=True)
                cosv = cs.ap[:, s * 64:s * 64 + 32]
                sinv = cs.ap[:, s * 64 + 32:s * 64 + 64]
                rt = ropet.ap
                P.op("dve", lambda e, pb=pb, cosv=cosv, rt=rt: e.tensor_tensor(out=rt[:, 0:32], in0=pb.ap[:, 0:32], in1=cosv, op=ALU.mult),
                     reads=[pb, cs], writes=[ropet])
                P.op("dve", lambda e, pb=pb, sinv=sinv, rt=rt: e.tensor_tensor(out=rt[:, 32:64], in0=pb.ap[:, 32:64], in1=sinv, op=ALU.mult),
                     reads=[pb, cs], writes=[ropet])
                P.op("dve", lambda e, rt=rt: e.tensor_tensor(out=lat_tok.ap[:, 512:544], in0=rt[:, 0:32], in1=rt[:, 32:64], op=ALU.subtract),
                     reads=[ropet], writes=[lat_tok])
                P.op("dve", lambda e, pb=pb, sinv=sinv, rt=rt: e.tensor_tensor(out=rt[:, 64:96], in0=pb.ap[:, 0:32], in1=sinv, op=ALU.mult),
                     reads=[pb, cs], writes=[ropet])
                P.op("dve", lambda e, pb=pb, cosv=cosv, rt=rt: e.tensor_tensor(out=rt[:, 96:128], in0=pb.ap[:, 32:64], in1=cosv, op=ALU.mult),
                     reads=[pb, cs], writes=[ropet])
                P.op("dve", lambda e, rt=rt: e.tensor_tensor(out=lat_tok.ap[:, 544:576], in0=rt[:, 64:96], in1=rt[:, 96:128], op=ALU.add),
                     reads=[ropet], writes=[lat_tok])
                for j in range(5):
                    bank = c.ps[s] if j < 4 else c.ps[4 + s]
                    col = (j % 4) * 128
                    P.op("pe", lambda e, j=j, bank=bank, col=col: e.transpose(out=bank.ap[:, col:col + 128],
                                                                              in_=lat_tok.ap[:, j * 128:(j + 1) * 128],
                                                                              identity=c.identf.ap),
                         reads=[lat_tok, c.identf], writes=[bank])
                st4 = Buf(latTs.ap[:, 0:4, s * 128:(s + 1) * 128], latTs.res)
                st1 = Buf(latTs.ap[:, 4, s * 128:(s + 1) * 128], latTs.res)
                P.op("act", lambda e, pa=pa, st4=st4: e.activation(out=st4.ap, in_=pa.ap.rearrange("p (j t) -> p j t", t=128), func=AF.Identity),
                     reads=[pa], writes=[st4])
                P.op("act", lambda e, pb=pb, st1=st1: e.activation(out=st1.ap, in_=pb.ap[:, 0:128], func=AF.Identity), reads=[pb], writes=[st1])
            for j in range(5 if not NO_SCATTER else 0):
                k = slot * 5 + j
                P.dma("pool", None, None, reads=[latTs, idx_sb], writes=[("dram", "lat", slot, j), ("scatter_token",)],
                      fn=lambda e, j=j, k=k: e.indirect_dma_start(out=lat_all, out_offset=bass.IndirectOffsetOnAxis(idx_sb.ap[:, k:k + 1], 0),
                                                                  in_=latTs.ap[:, j, :], in_offset=None,
                                                                  bounds_check=e.to_reg(8 * 640 - 1), oob_is_err=False))
        pass


L1_IN = {"wada1": [48, 128, 8192], "bada1": [128, 3 * D], "bwin": [36, 128, 8192], "qg": [128, 8],
         "wqb": [16, 128, 8 * 768], "kvwb": [8, 128, 4 * 2048], "wout1": [32, 128, 8192], "lng1": [128, D], "lnb1": [128, D],
         "csq": [NSLOT, 64, 2, T], "mask": [4, 128, T], "mflag": [128, 2]}


def l1_body(nc, P, big, banks, t, nslot=NSLOT, nheads=NH):
    xt, lat_all, cT_d, wada, bada, bwin, qg_d = t["x1"], t["lat_all"], t["cT"], t["wada1"], t["bada1"], t["bwin"], t["qg"]
    wqb, kvwb, wout, lng_d, lnb_d, csq_d, mask_d, mflag_d = t["wqb"], t["kvwb"], t["wout1"], t["lng1"], t["lnb1"], t["csq"], t["mask"], t["mflag"]
    ident_d, out_d, gate_d, sz_d, og_d = t["ident"], t["out"], t["gate_s1"], t["sz_s"], t["og_s"]
    QSCALE = 192.0 ** -0.5
    if True:
        S = Sbuf(big, SB_BYTES)
        c = Ctx()
        setup_common(nc, P, S, c)
        c.ps = [Buf(b[:], [("ps", i)]) for i, b in enumerate(banks)]
        alloc_small(c, S)
        qg = S.new(F32, [128, 8])
        onesb = S.new(BF16, [128, 128])
        ring = WRing(P, S, 2, 8192)
        KB = 1024
        r0 = S.alloc(S.nbytes - S.top - 512)
        print("L1 pool bytes", S.nbytes - r0)
        hT = S.view(r0, BF16, [128, 32, 512])
        xbuf = S.view(r0 + 32 * KB, F32, [128, D])
        ybuf = S.view(r0 + 48 * KB, BF16, [128, D])
        cqf = S.view(r0 + 56 * KB, F32, [128, 8, 512])
        cqnT = S.view(r0 + 72 * KB, BF16, [128, 8, NSLOT * T])
        zst = [S.view(r0 + 104 * KB + i * KB, BF16, [128, 512]) for i in range(2)]
        sqb = S.view(r0 + 106 * KB, F32, [128, 512])
        rinvb = S.view(r0 + 108 * KB, F32, [128, 512])
        c.scb = S.view(r0, BF16, [128, 32, 128])
        latb = S.view(r0, BF16, [128, 5, SEQ])
        kTh = S.view(r0 + 40 * KB, BF16, [128, SEQ])
        Vh = S.view(r0 + 48 * KB, BF16, [128, 32, 128])
        qn = S.view(r0 + 56 * KB, BF16, [128, NSLOT * T])
        qr = S.view(r0 + 60 * KB, BF16, [128, NSLOT * T])
        ropeT = S.view(r0 + 64 * KB, F32, [128, 4 * 512])
        pT = [S.view(r0 + 104 * KB + i * KB, BF16, [128, 512]) for i in range(2)] + \
             [S.view(r0 + 132 * KB + i * KB, BF16, [128, 512]) for i in range(2)]
        masks = S.view(r0 + 106 * KB, BF16, [128, 4, 512])
        mflag = S.view(r0 + 110 * KB, F32, [128, 2])
        szh = S.view(r0 + 114 * KB, BF16, [128, NSLOT * T])
        ogh = S.view(r0 + 118 * KB, BF16, [128, NSLOT * T])
        rs = S.view(r0 + 122 * KB, F32, [128, 512])
        of = S.view(r0 + 124 * KB, F32, [128, 512])
        csq = S.view(r0 + 126 * KB, F32, [128, 2, 512])
        qraw = S.view(r0 + 130 * KB, F32, [128, 512])
        assert 134 * KB <= S.nbytes - r0, (S.nbytes - r0)
        ogs = S.view(r0, BF16, [128, 64, 512])
        r_all = [S.view(r0 + 64 * KB + s * 16 * KB, F32, [128, D]) for s in range(4)]
        assert 128 * KB <= S.nbytes - r0

        init_consts(P, c, ident_d)
        P.op("dve", lambda e: e.memset(onesb.ap, 1.0), writes=[onesb])
        P.dma("sp", qg.ap, qg_d, writes=[qg])
        compute_mod(P, c, ring, wada, bada, cT_d, gate_d, c.shiftT, c.scale1T)
        gate_keys = [("dram", "gate", n) for n in range(8)]

        for slot in range(nslot):
            for s in range(4):
                stage_a(P, c, xt[slot, s * 128:(s + 1) * 128, :], 128, xbuf, ybuf,
                        lambda j, s=s: sub(hT, hT.ap[:, j, s * 128:(s + 1) * 128], j * 1024 + s * 256, j * 1024 + (s + 1) * 256),
                        s * 128, c.scale1T, c.shiftT, xkeys=[("dram", "x1", slot, s)])
            for wt in range(4):
                w = ring.load(bwin[wt], 8192)
                w3 = w.ap.rearrange("p (a b) -> p a b", b=256)
                for hh in range(2):
                    j = wt * 2 + hh
                    pz = c.ps[2 + (j % 4)]
                    for k in range(32):
                        hk = sub(hT, hT.ap[:, k, :], k * 1024, (k + 1) * 1024)
                        P.op("pe", lambda e, pz=pz, hh=hh, k=k, w3=w3, hk=hk: e.matmul(
                            pz.ap, lhsT=w3[:, k, hh * 128:(hh + 1) * 128], rhs=hk.ap, start=(k == 0), stop=(k == 31)),
                             reads=[w, hk], writes=[pz])
                    cj = sub(cqf, cqf.ap[:, j, :], j * 2048, (j + 1) * 2048)
                    P.op("act", lambda e, pz=pz, cj=cj: e.activation(out=cj.ap, in_=pz.ap, func=AF.Identity), reads=[pz], writes=[cj])
                    P.op("act", lambda e, pz=pz: e.activation(out=sqb.ap, in_=pz.ap, func=AF.Square), reads=[pz], writes=[sqb])
                    P.op("pe", lambda e, j=j: e.matmul(c.ps[6].ap, lhsT=c.onesf.ap, rhs=sqb.ap, start=(j == 0), stop=(j == 7)),
                         reads=[sqb, c.onesf], writes=[c.ps[6]])
            P.op("act", lambda e: e.activation(out=sqb.ap, in_=c.ps[6].ap, func=AF.Sqrt, bias=c.eps6.ap, scale=1.0 / 1024),
                 reads=[c.ps[6], c.eps6], writes=[sqb])
            P.op("dve", lambda e: e.reciprocal(out=rinvb.ap, in_=sqb.ap), reads=[sqb], writes=[rinvb])
            for j in range(8):
                cj = sub(cqf, cqf.ap[:, j, :], j * 2048, (j + 1) * 2048)
                dst = sub(cqnT, cqnT.ap[:, j, slot * T:(slot + 1) * T], j * 4096 + slot * 1024, j * 4096 + (slot + 1) * 1024)
                P.op("dve", lambda e, cj=cj, dst=dst, j=j: e.scalar_tensor_tensor(
                    out=dst.ap, in0=cj.ap, scalar=qg.ap[:, j:j + 1], in1=rinvb.ap, op0=ALU.mult, op1=ALU.mult),
                     reads=[cj, qg, rinvb], writes=[dst])
            for wt in range(32):
                w = ring.load(bwin[4 + wt], 8192)
                w3 = w.ap.rearrange("p (a b) -> p a b", b=256)
                for hh in range(2):
                    j = wt * 2 + hh
                    pz = c.ps[2 + (j % 4)]
                    for k in range(32):
                        hk = sub(hT, hT.ap[:, k, :], k * 1024, (k + 1) * 1024)
                        P.op("pe", lambda e, pz=pz, hh=hh, k=k, w3=w3, hk=hk: e.matmul(
                            pz.ap, lhsT=w3[:, k, hh * 128:(hh + 1) * 128], rhs=hk.ap, start=(k == 0), stop=(k == 31)),
                             reads=[w, hk], writes=[pz])
                    zb = zst[j % 2]
                    P.op("act", lambda e, pz=pz, zb=zb: e.activation(out=zb.ap, in_=pz.ap, func=AF.Silu), reads=[pz], writes=[zb])
                    P.dma("sp", sz_d[j, :, slot * T:(slot + 1) * T], zb.ap, reads=[zb], writes=[("dram", "sz", j, slot)])

        lat_keys = [("dram", "lat", st_, j_) for st_ in range(NSTEP) for j_ in range(5)]
        for kt in range(8):
            P.dma("pool", latb.ap[:, :, kt * T:(kt + 1) * T], lat_all[kt * 640:(kt + 1) * 640, :].rearrange("(j p) t -> p j t", p=128),
                  reads=lat_keys, writes=[latb])
        for m in range(4):
            mj = sub(masks, masks.ap[:, m, :], m * 1024, (m + 1) * 1024)
            P.dma("pool", mj.ap, mask_d[m], writes=[mj])
        P.dma("sp", mflag.ap, mflag_d, writes=[mflag])
        for h in range(nheads):
            if h % 8 == 0:
                wkv = ring.load(kvwb[h // 8], 8192, slot=0)
                wkv3 = wkv.ap.rearrange("p (a b) -> p a b", b=2048)
            if h % 4 == 0:
                wq = ring.load(wqb[h // 4], 8 * 768, slot=1)
                wq3 = wq.ap.rearrange("p (a b) -> p a b", b=768)
            P.dma("sp", szh.ap, sz_d[h], reads=[("dram", "sz", h, s_) for s_ in range(nslot)], writes=[szh])
            ko = (h % 8) * 256
            qo = (h % 4) * 192
            for kt in range(8):
                bank = c.ps[kt % 2]
                for kc in range(4):
                    P.op("pe", lambda e, bank=bank, kc=kc, kt=kt, ko=ko, wkv3=wkv3: e.matmul(
                        bank.ap, lhsT=wkv3[:, kc, ko:ko + 128], rhs=latb.ap[:, kc, kt * 512:(kt + 1) * 512],
                        start=(kc == 0), stop=(kc == 3)), reads=[wkv, latb], writes=[bank])
                dst = sub(kTh, kTh.ap[:, kt * 512:(kt + 1) * 512], kt * 1024, (kt + 1) * 1024)
                P.op("act", lambda e, bank=bank, dst=dst: e.activation(out=dst.ap, in_=bank.ap, func=AF.Identity), reads=[bank], writes=[dst])
            for g in range(8):
                bank = c.ps[g % 2]
                for q4 in range(4):
                    kc2 = g * 4 + q4
                    for kc in range(4):
                        P.op("pe", lambda e, bank=bank, kc=kc, kc2=kc2, q4=q4, ko=ko, wkv3=wkv3: e.matmul(
                            bank.ap[:, q4 * 128:(q4 + 1) * 128], lhsT=latb.ap[:, kc, kc2 * 128:(kc2 + 1) * 128],
                            rhs=wkv3[:, kc, ko + 128:ko + 256], start=(kc == 0), stop=(kc == 3)),
                             reads=[wkv, latb], writes=[bank])
                dst = Buf(Vh.ap[:, g * 4:(g + 1) * 4, :], list(range(Vh.res[0] + g * 2, Vh.res[0] + g * 2 + 2)))
                P.op("dve", lambda e, bank=bank, dst=dst: e.tensor_copy(out=dst.ap, in_=bank.ap.rearrange("p (a b) -> p a b", b=128)),
                     reads=[bank], writes=[dst])
            for s in range(nslot):
                bank = c.ps[2]
                for kc in range(8):
                    P.op("pe", lambda e, bank=bank, kc=kc, s=s, qo=qo, wq3=wq3: e.matmul(
                        bank.ap, lhsT=wq3[:, kc, qo:qo + 128], rhs=cqnT.ap[:, kc, s * T:(s + 1) * T], start=(kc == 0), stop=(kc == 7)),
                         reads=[wq, cqnT], writes=[bank])
                dst = sub(qn, qn.ap[:, s * T:(s + 1) * T], s * 1024, (s + 1) * 1024)
                P.op("act", lambda e, bank=bank, dst=dst: e.activation(out=dst.ap, in_=bank.ap, func=AF.Identity, scale=QSCALE),
                     reads=[bank], writes=[dst])
                bank = c.ps[3]
                for kc in range(8):
                    P.op("pe", lambda e, bank=bank, kc=kc, s=s, qo=qo, wq3=wq3: e.matmul(
                        bank.ap[0:64, :], lhsT=wq3[:, kc, qo + 128:qo + 192], rhs=cqnT.ap[:, kc, s * T:(s + 1) * T], start=(kc == 0), stop=(kc == 7)),
                         reads=[wq, cqnT], writes=[bank])
                P.dma("sp", csq.ap[0:64], csq_d[s], writes=[csq])
                cosT = csq.ap[:, 0, :]
                sinT = csq.ap[:, 1, :]
                tA, tB, tC, tD = [ropeT.ap[:, i * 512:(i + 1) * 512] for i in range(4)]
                P.op("act", lambda e, bank=bank: e.activation(out=qraw.ap[0:64], in_=bank.ap[0:64, :], func=AF.Identity, scale=QSCALE),
                     reads=[bank], writes=[qraw])
                P.op("dve", lambda e, cosT=cosT, tA=tA: e.tensor_tensor(out=tA[0:32], in0=qraw.ap[0:32], in1=cosT[0:32], op=ALU.mult),
                     reads=[qraw, csq], writes=[ropeT])
                P.op("dve", lambda e, sinT=sinT, tB=tB: e.tensor_tensor(out=tB[0:32], in0=qraw.ap[32:64], in1=sinT[32:64], op=ALU.mult),
                     reads=[qraw, csq], writes=[ropeT])
                P.op("dve", lambda e, sinT=sinT, tC=tC: e.tensor_tensor(out=tC[32:64], in0=qraw.ap[0:32], in1=sinT[0:32], op=ALU.mult),
                     reads=[qraw, csq], writes=[ropeT])
                P.op("dve", lambda e, cosT=cosT, tD=tD: e.tensor_tensor(out=tD[32:64], in0=qraw.ap[32:64], in1=cosT[32:64], op=ALU.mult),
                     reads=[qraw, csq], writes=[ropeT])
                dst = sub(qr, qr.ap[:, s * T:(s + 1) * T], s * 1024, (s + 1) * 1024)
                P.op("dve", lambda e, tA=tA, tB=tB, dst=dst: e.tensor_tensor(out=dst.ap[0:32], in0=tA[0:32], in1=tB[0:32], op=ALU.subtract),
                     reads=[ropeT], writes=[dst])
                P.op("dve", lambda e, tC=tC, tD=tD, dst=dst: e.tensor_tensor(out=dst.ap[32:64], in0=tC[32:64], in1=tD[32:64], op=ALU.add),
                     reads=[ropeT], writes=[dst])
            for s in range(nslot):
                nkc = 4 * PADLEN[s]
                po, psum_ = c.ps[4], c.ps[5]
                qs = sub(qn, qn.ap[:, s * T:(s + 1) * T], s * 1024, (s + 1) * 1024)
                qrs = sub(qr, qr.ap[0:64, s * T:(s + 1) * T], s * 1024, (s + 1) * 1024)
                LA = 3
                sbanks = [c.ps[6], c.ps[7], c.ps[3], c.ps[2]]

                def emit_qk(kc2):
                    bank = sbanks[kc2 % 4]
                    kk = sub(kTh, kTh.ap[:, kc2 * 128:(kc2 + 1) * 128], kc2 * 256, (kc2 + 1) * 256)
                    P.op("pe", lambda e, bank=bank, kk=kk, qs=qs: e.matmul(bank.ap, lhsT=kk.ap, rhs=qs.ap, start=True, stop=False),
                         reads=[kk, qs], writes=[bank])
                    P.op("pe", lambda e, bank=bank, kc2=kc2, qrs=qrs: e.matmul(
                        bank.ap, lhsT=latb.ap[0:64, 4, kc2 * 128:(kc2 + 1) * 128], rhs=qrs.ap, start=False, stop=True),
                         reads=[latb, qrs], writes=[bank])

                for kc2 in range(min(LA, nkc)):
                    emit_qk(kc2)
                for kc2 in range(nkc):
                    if kc2 + LA < nkc:
                        emit_qk(kc2 + LA)
                    bank = sbanks[kc2 % 4]
                    pt = pT[kc2 % 4]
                    P.op("act", lambda e, bank=bank, pt=pt: e.activation(out=pt.ap, in_=bank.ap, func=AF.Exp), reads=[bank], writes=[pt])
                    mi = kc2 - (nkc - 8)
                    if mi >= 0:
                        mop = ALU.max if mi < 4 else ALU.mult
                        P.op("dve", lambda e, pt=pt, mi=mi, mop=mop, s=s: e.scalar_tensor_tensor(
                            out=pt.ap, in0=masks.ap[:, mi % 4, :], scalar=mflag.ap[:, s % 2:s % 2 + 1], in1=pt.ap, op0=mop, op1=ALU.mult),
                             reads=[pt, masks, mflag], writes=[pt])
                    vv = Buf(Vh.ap[:, kc2, :], [Vh.res[0] + kc2 // 2])
                    P.op("pe", lambda e, vv=vv, pt=pt, kc2=kc2, nkc=nkc: e.matmul(po.ap, lhsT=vv.ap, rhs=pt.ap, start=(kc2 == 0), stop=(kc2 == nkc - 1)),
                         reads=[vv, pt], writes=[po])
                    P.op("pe", lambda e, pt=pt, kc2=kc2, nkc=nkc: e.matmul(psum_.ap, lhsT=onesb.ap, rhs=pt.ap, start=(kc2 == 0), stop=(kc2 == nkc - 1)),
                         reads=[onesb, pt], writes=[psum_])
                P.op("dve", lambda e: e.reciprocal(out=rs.ap, in_=psum_.ap), reads=[psum_], writes=[rs])
                P.op("dve", lambda e: e.tensor_tensor(out=of.ap, in0=po.ap, in1=rs.ap, op=ALU.mult), reads=[po, rs], writes=[of])
                dst = sub(ogh, ogh.ap[:, s * T:(s + 1) * T], s * 1024, (s + 1) * 1024)
                P.op("dve", lambda e, dst=dst, s=s: e.tensor_tensor(out=dst.ap, in0=of.ap, in1=szh.ap[:, s * T:(s + 1) * T], op=ALU.mult),
                     reads=[of, szh], writes=[dst])
            P.dma("sp", og_d[h], ogh.ap, reads=[ogh], writes=[("dram", "og", h)])

        for slot in range(nslot):
            for hc in range(4):
                blk = Buf(ogs.ap[:, hc * 16:(hc + 1) * 16, :], list(range(ogs.res[0] + hc * 32, ogs.res[0] + (hc + 1) * 32)))
                P.dma("sp", blk.ap, og_d[hc * 16:(hc + 1) * 16, :, slot * T:(slot + 1) * T].rearrange("j p t -> p j t"),
                      reads=[("dram", "og", h_) for h_ in range(hc * 16, (hc + 1) * 16)], writes=[blk])
            for s in range(4):
                P.dma("sp", r_all[s].ap, xt[slot, s * 128:(s + 1) * 128, :], reads=[("dram", "x1", slot, s)], writes=[r_all[s]])
            outproj_residual(P, c, ring, wout, 4,
                             lambda k, s: sub(ogs, ogs.ap[:, k, s * 128:(s + 1) * 128], k * 1024 + s * 256, k * 1024 + (s + 1) * 256),
                             r_all, gate_d, gate_keys)
            final_ln(P, c, r_all, lng_d, lnb_d)
            for s in range(4):
                P.dma("sp", out_d[slot, s * 128:(s + 1) * 128, :], r_all[s].ap, reads=[r_all[s]], writes=[("dram", "out", slot, s)])
        pass


def tile_w(W, kc_per_tile, cols):
    K, N = W.shape
    g = K // (128 * kc_per_tile)
    A = W.reshape(g, kc_per_tile, 128, N // cols, cols)
    A = A.transpose(3, 0, 2, 1, 4)
    return np.ascontiguousarray(A).reshape(N // cols, g, 128, kc_per_tile * cols)


def fmaj(v):
    return np.ascontiguousarray(v.reshape(-1, 128).T)


def rep(v):
    return np.ascontiguousarray(np.broadcast_to(v[None, :], (128, v.shape[0])))


def rope_tables(pos):
    half = 32
    inv_freq = (10000.0 ** (-np.arange(half, dtype=np.float32) / half)).astype(np.float32)
    ang = pos.astype(np.float32)[:, None] * inv_freq[None, :]
    return np.cos(ang).astype(np.float32), np.sin(ang).astype(np.float32)


def l0_inputs(inp):
    x = inp["x"]
    a_w_in = inp["a_w_in"][0]
    ag = np.concatenate([a_w_in[:, 0:D].reshape(D, 32, 1, 128), a_w_in[:, D:2 * D].reshape(D, 32, 1, 128)], axis=2).reshape(D, 32 * 256)
    win_ag = tile_w(ag, 32, 256).reshape(32, 128, 8192)
    win_z = tile_w(a_w_in[:, 2 * D:], 32, 256).reshape(16, 128, 8192)
    win = np.concatenate([win_ag, win_z], axis=0)
    shared = {
        "wada": tile_w(inp["w_ada"][0], 16, 512).reshape(48, 128, 8192),
        "bada": rep(inp["b_ada"][0]),
        "win": win,
        "wdw": np.ascontiguousarray(inp["a_w_dw"][0].T.reshape(32, 128, 31).transpose(1, 0, 2)).reshape(128, 32 * 31),
        "bdw": fmaj(inp["a_b_dw"][0]),
        "cng": fmaj(inp["a_norm_g"][0]),
        "cnb": fmaj(inp["a_norm_b"][0]),
        "wout": tile_w(inp["a_w_out"][0], 16, 512).reshape(16, 128, 8192),
        "lng": rep(inp["ln_g"][0]),
        "lnb": rep(inp["ln_b"][0]),
        "kvwa": tile_w(inp["kv_w_a"], 8, 576).reshape(4, 128, 8 * 576),
        "kvg": rep(inp["kv_norm_g"]),
        "ident": np.eye(128, dtype=np.float32),
    }
    maps = []
    for core in range(8):
        b, par = core // 2, core % 2
        tiles = TILES[par] + TILES[1 - par]
        xt = np.stack([x[b, t * T:(t + 1) * T] for t in tiles])
        xh = np.stack([x[b, t * T - 32:t * T] if t > 0 else np.zeros((32, D), np.float32) for t in tiles])
        hm = np.array([1.0 if t > 0 else 0.0 for t in tiles], np.float32)
        cs = []
        for t in tiles:
            co, si = rope_tables(np.arange(t * T, (t + 1) * T))
            cs.append(np.concatenate([co, si], axis=1).reshape(4, 128, 64))
        m = dict(shared)
        idx = np.zeros((128, NSTEP * 5), np.uint32)
        for st_, t in enumerate(tiles):
            for j in range(5):
                idx[:, st_ * 5 + j] = t * 640 + j * 128 + np.arange(128)
        m.update({"xt": xt, "xh": xh, "hmask": rep(hm), "cT": fmaj(inp["c"][b]), "cs": np.stack(cs).astype(np.float32), "idx": idx})
        maps.append(m)
    return maps


def causal_masks():
    k = np.arange(128)[:, None]
    q = np.arange(512)[None, :]
    return np.stack([(q >= 128 * j + k).astype(np.float32) for j in range(4)])


def mask_flags(par):
    fl = [1.0 if (TILES[par][sl] + 1) == PADLEN[sl] else 0.0 for sl in range(2)]
    for sl in range(NSLOT):
        assert fl[sl % 2] == (1.0 if (TILES[par][sl] + 1) == PADLEN[sl] else 0.0)
        assert TILES[par][sl] + 1 in (PADLEN[sl], PADLEN[sl] - 1)
    return rep(np.array(fl, np.float32))


def l1_inputs(inp):
    b_w_in = inp["b_w_in"][0]
    shared = {
        "wada1": tile_w(inp["w_ada"][1], 16, 512).reshape(48, 128, 8192),
        "bada1": rep(inp["b_ada"][1]),
        "bwin": tile_w(b_w_in, 32, 256).reshape(36, 128, 8192),
        "qg": fmaj(inp["b_q_norm_g"][0]),
        "wqb": tile_w(inp["b_w_qb"][0], 8, 768).reshape(16, 128, 8 * 768),
        "kvwb": tile_w(inp["kv_w_b"], 4, 2048).reshape(8, 128, 4 * 2048),
        "wout1": tile_w(inp["b_w_out"][0], 16, 512).reshape(32, 128, 8192),
        "lng1": rep(inp["ln_g"][1]),
        "lnb1": rep(inp["ln_b"][1]),
        "mask": causal_masks(),
    }
    maps = []
    for core in range(8):
        par = core % 2
        csq = []
        for t in TILES[par]:
            co, si = rope_tables(np.arange(t * T, (t + 1) * T))
            cs1 = np.stack([co.T, si.T], axis=1)
            csq.append(np.concatenate([cs1, cs1], axis=0))
        m = dict(shared)
        m.update({"csq": np.stack(csq).astype(np.float32), "mflag": mask_flags(par)})
        maps.append(m)
    return maps


def build_fused():
    nc = bass.Bass("TRN2", target_bir_lowering=False, dynamic_dma_scratch_size=8192)
    t = {}
    for name, shape in list(L0_IN.items()) + list(L1_IN.items()):
        t[name] = nc.dram_tensor(name, shape, F32, kind="ExternalInput").ap()
    t["idx"] = nc.dram_tensor("idx", [128, NSTEP * 5], U32, kind="ExternalInput").ap()
    t["out"] = nc.dram_tensor("out", [NSLOT, T, D], F32, kind="ExternalOutput").ap()
    t["x1"] = nc.dram_tensor("x1_s", [NSTEP, T, D], F32, kind="Internal").ap()
    t["lat_all"] = nc.dram_tensor("lat_all", [8 * 640, T], F32, kind="Internal").ap()
    t["gate_s"] = nc.dram_tensor("gate_s", [128, D], F32, kind="Internal").ap()
    t["gate_s1"] = nc.dram_tensor("gate_s1", [128, D], F32, kind="Internal").ap()
    t["sz_s"] = nc.dram_tensor("sz_s", [64, 128, NSLOT * T], BF16, kind="Internal").ap()
    t["og_s"] = nc.dram_tensor("og_s", [64, 128, NSLOT * T], BF16, kind="Internal").ap()
    import contextlib
    with contextlib.ExitStack() as es:
        big = es.enter_context(nc.sbuf_tensor("big", [128, SB_BYTES // 4], F32))
        t["idx_sb"] = es.enter_context(nc.sbuf_tensor("idx_sb", [128, NSTEP * 5], U32))
        banks = [es.enter_context(nc.psum_tensor("ps%d" % i, [128, 512], F32)) for i in range(8)]
        P = Prog(nc)
        l0_body(nc, P, big, banks, t)
        l1_body(nc, P, big, banks, t)
        P.emit()
    return nc


def kernel(**inputs):
    inp = {k: np.asarray(v, dtype=np.float32) for k, v in inputs.items()}
    m0 = l0_inputs(inp)
    m1 = l1_inputs(inp)
    maps = []
    for core in range(8):
        m = dict(m0[core])
        m.update(m1[core])
        maps.append(m)
    nc = build_fused()
    res = run_bass_kernel_spmd(nc, maps, core_ids=list(range(8))).results
    out = np.zeros((BATCH, SEQ, D), np.float32)
    for core in range(8):
        b, par = core // 2, core % 2
        for sl, tl in enumerate(TILES[par]):
            out[b, tl * T:(tl + 1) * T] = res[core]["out"][sl]
    return out
```
